# Optimizing a Trainium2 kernel written in Bass

```python
import math
import jax, jax.numpy as jnp
from jax import lax
import numpy as np

D_MODEL = 1024
BATCH = 2
SEQ = 8192
DEPTH = 2

GRID_W = 64
CTX_LEN = 256
EPS = 1e-6
ROPE_BASE = 10000.0
HEAD_DIM = 64
MIX_WIDTH = D_MODEL
NA_WIDTH = D_MODEL // 2
NA_HEADS = NA_WIDTH // HEAD_DIM
NA_KH_MAX = 8
NA_KW = 16
DIFF_WIDTH = D_MODEL // 4
DIFF_HEADS = 4
DIFF_V_DIM = DIFF_WIDTH // DIFF_HEADS
DIFF_QK_DIM = DIFF_V_DIM // 2
FNET_WIDTH = D_MODEL // 4
FNET_GROUPS = 4
FNET_GROUP_DIM = FNET_WIDTH // FNET_GROUPS
NA_Q0 = 0
NA_K0 = NA_Q0 + NA_WIDTH
NA_V0 = NA_K0 + NA_WIDTH
DQ0 = NA_V0 + NA_WIDTH
DK0 = DQ0 + DIFF_HEADS * 2 * DIFF_QK_DIM
DV0 = DK0 + DIFF_HEADS * 2 * DIFF_QK_DIM
FN0 = DV0 + DIFF_WIDTH
IN_WIDTH = FN0 + FNET_WIDTH
D_FF = 2816
CONV_W = 3
Q_BLOCK = 128

kernel_name = 'hybrid_na_diff_fnet_prefix_block'


def _rmsnorm(x, g):
    x32 = x.astype(jnp.float32)
    y = x32 * lax.rsqrt(jnp.mean(x32 * x32, axis=-1, keepdims=True) + EPS)
    return (y * g.astype(jnp.float32)).astype(x.dtype)


def _modulate(h, shift, scale):
    return h * (1 + scale) + shift


def _rope_1d(x, pos):
    m = x.shape[-1]
    inv = ROPE_BASE ** (-jnp.arange(0, m, 2, dtype=jnp.float32) / m)
    ang = pos.astype(jnp.float32)[:, None] * inv[None, :]
    cos = jnp.cos(ang)[None, :, None, :]
    sin = jnp.sin(ang)[None, :, None, :]
    x32 = x.astype(jnp.float32)
    x1, x2 = x32[..., : m // 2], x32[..., m // 2:]
    return jnp.concatenate([x1 * cos - x2 * sin, x1 * sin + x2 * cos], axis=-1).astype(x.dtype)


def _axial_rope(x, rows, cols):
    d = x.shape[-1]
    return jnp.concatenate([_rope_1d(x[..., : d // 2], rows), _rope_1d(x[..., d // 2:], cols)], axis=-1)


def _split_heads(p, na_q_g, na_k_g, d_q_g, d_k_g):
    B, L, _ = p.shape
    na_q = _rmsnorm(p[..., NA_Q0:NA_K0].reshape(B, L, NA_HEADS, HEAD_DIM), na_q_g)
    na_k = _rmsnorm(p[..., NA_K0:NA_V0].reshape(B, L, NA_HEADS, HEAD_DIM), na_k_g)
    na_v = p[..., NA_V0:DQ0].reshape(B, L, NA_HEADS, HEAD_DIM)
    dq = _rmsnorm(p[..., DQ0:DK0].reshape(B, L, DIFF_HEADS, 2, DIFF_QK_DIM), d_q_g)
    dk = _rmsnorm(p[..., DK0:DV0].reshape(B, L, DIFF_HEADS, 2, DIFF_QK_DIM), d_k_g)
    dv = p[..., DV0:FN0].reshape(B, L, DIFF_HEADS, DIFF_V_DIM)
    fu = p[..., FN0:IN_WIDTH]
    return na_q, na_k, na_v, dq, dk, dv, fu


def _neighbourhood_attention(q, k, v, kc, vc, rpb, n_rows):
    B, S, H, Dh = q.shape
    kh = min(NA_KH_MAX, n_rows)
    n_keys = kh * NA_KW
    scale = Dh ** -0.5
    cols = jnp.arange(GRID_W)
    col_start = jnp.clip(cols - NA_KW // 2, 0, GRID_W - NA_KW)
    key_cols = col_start[:, None] + jnp.arange(NA_KW)[None, :]
    dc = key_cols - cols[:, None] + (NA_KW - 1)
    q_rows = q.reshape(B, n_rows, GRID_W, H, Dh).transpose(1, 0, 2, 3, 4)

    def row_block(args):
        r, q_r = args
        row_start = jnp.clip(r - kh // 2, 0, n_rows - kh)
        key_rows = row_start + jnp.arange(kh)
        idx = (key_rows[None, :, None] * GRID_W + key_cols[:, None, :]).reshape(GRID_W, n_keys)
        k_g = k[:, idx]
        v_g = v[:, idx]
        dr = key_rows - r + (NA_KH_MAX - 1)
        bias = rpb[:, dr[None, :, None], dc[:, None, :]].reshape(H, GRID_W, n_keys)
        s_loc = jnp.einsum('bqhd,bqnhd->bhqn', q_r, k_g).astype(jnp.float32) * scale + bias[None].astype(jnp.float32)
        s_ctx = jnp.einsum('bqhd,bchd->bhqc', q_r, kc).astype(jnp.float32) * scale
        p = jax.nn.softmax(jnp.concatenate([s_loc, s_ctx], axis=-1), axis=-1).astype(v.dtype)
        return (jnp.einsum('bhqn,bqnhd->bqhd', p[..., :n_keys], v_g)
                + jnp.einsum('bhqc,bchd->bqhd', p[..., n_keys:], vc))

    out = lax.map(row_block, (jnp.arange(n_rows), q_rows))
    return out.transpose(1, 0, 2, 3, 4).reshape(B, S, H, Dh)


def _dense_attention(q, k, v):
    s = jnp.einsum('bqhd,bkhd->bhqk', q, k).astype(jnp.float32) * q.shape[-1] ** -0.5
    p = jax.nn.softmax(s, axis=-1).astype(v.dtype)
    return jnp.einsum('bhqk,bkhd->bqhd', p, v)


def _diff_attend(q1, q2, k1, k2, v, lam):
    scale = q1.shape[-1] ** -0.5
    s1 = jnp.einsum('bqhd,bkhd->bhqk', q1, k1).astype(jnp.float32) * scale
    s2 = jnp.einsum('bqhd,bkhd->bhqk', q2, k2).astype(jnp.float32) * scale
    p = jax.nn.softmax(s1, axis=-1) - lam * jax.nn.softmax(s2, axis=-1)
    return jnp.einsum('bhqk,bkhe->bqhe', p.astype(v.dtype), v)


def _diff_attention_latent(q1, q2, k1, k2, v, k1c, k2c, vc, lam):
    B, S, H, d = q1.shape
    kk1 = jnp.concatenate([k1, k1c], axis=1)
    kk2 = jnp.concatenate([k2, k2c], axis=1)
    vv = jnp.concatenate([v, vc], axis=1)
    nb = S // Q_BLOCK
    q1b = q1.reshape(B, nb, Q_BLOCK, H, d).transpose(1, 0, 2, 3, 4)
    q2b = q2.reshape(B, nb, Q_BLOCK, H, d).transpose(1, 0, 2, 3, 4)
    out = lax.map(lambda a: _diff_attend(a[0], a[1], kk1, kk2, vv, lam), (q1b, q2b))
    return out.transpose(1, 0, 2, 3, 4).reshape(B, S, H, v.shape[-1])


def _fourier(u):
    B, L, _ = u.shape
    g = u.reshape(B, L, FNET_GROUPS, FNET_GROUP_DIM).astype(jnp.float32)
    f = jnp.fft.fftn(g, axes=(1, 3), norm='ortho').real
    return f.reshape(B, L, FNET_WIDTH).astype(u.dtype)


def _merge(na_o, diff_o, four_u, subln_g, lam_init, w_four, w_out):
    B, L = na_o.shape[:2]
    diff_o = _rmsnorm(diff_o, subln_g) * (1.0 - lam_init)
    four = _fourier(four_u) @ w_four
    y = jnp.concatenate([na_o.reshape(B, L, NA_WIDTH), diff_o.reshape(B, L, DIFF_WIDTH), four], axis=-1)
    return y @ w_out


def _conv_ffn(h, w_up, conv_w, conv_b, w_down):
    u = h @ w_up
    up = jnp.pad(u, ((0, 0), (1, 1), (0, 0)))
    u = up[:, :-2] * conv_w[0] + up[:, 1:-1] * conv_w[1] + up[:, 2:] * conv_w[2] + conv_b
    gate, val = jnp.split(u, 2, axis=-1)
    return (jax.nn.silu(gate) * val) @ w_down


def setup_inputs(seed: int = 0) -> dict:
    key = jax.random.key(seed)
    ks = jax.random.split(key, 22)
    D = D_MODEL

    def nrm(k, shape, s):
        return jax.random.normal(k, shape, jnp.float32) * s

    centre = (jnp.arange(CONV_W) == CONV_W // 2).astype(jnp.float32)[None, :, None]
    return {
        'x': nrm(ks[0], (BATCH, SEQ, D), 1.0),
        'c': nrm(ks[1], (BATCH, D), 1.0),
        'ctx': nrm(ks[2], (BATCH, CTX_LEN, D), 1.0),
        'c_ctx': nrm(ks[3], (D,), 1.0),
        'w_mod': nrm(ks[4], (DEPTH, D, 6 * D), 0.5 * D ** -0.5),
        'b_mod': nrm(ks[5], (DEPTH, 6 * D), 0.02),
        'norm1_g': 1.0 + nrm(ks[6], (DEPTH, D), 0.05),
        'w_in': nrm(ks[7], (DEPTH, D, IN_WIDTH), D ** -0.5),
        'na_q_g': 1.0 + nrm(ks[8], (DEPTH, HEAD_DIM), 0.05),
        'na_k_g': 1.0 + nrm(ks[9], (DEPTH, HEAD_DIM), 0.05),
        'na_rpb': nrm(ks[10], (DEPTH, NA_HEADS, 2 * NA_KH_MAX - 1, 2 * NA_KW - 1), 0.1),
        'diff_q_g': 1.0 + nrm(ks[11], (DEPTH, DIFF_QK_DIM), 0.05),
        'diff_k_g': 1.0 + nrm(ks[12], (DEPTH, DIFF_QK_DIM), 0.05),
        'diff_lambda': nrm(ks[13], (DEPTH, 4, DIFF_QK_DIM), 0.1),
        'diff_subln_g': 1.0 + nrm(ks[14], (DEPTH, DIFF_V_DIM), 0.05),
        'w_fourier': nrm(ks[15], (DEPTH, FNET_WIDTH, FNET_WIDTH), FNET_WIDTH ** -0.5),
        'w_out': nrm(ks[16], (DEPTH, MIX_WIDTH, D), MIX_WIDTH ** -0.5),
        'norm2_g': 1.0 + nrm(ks[17], (DEPTH, D), 0.05),
        'w_up': nrm(ks[18], (DEPTH, D, 2 * D_FF), D ** -0.5),
        'conv_w': centre + nrm(ks[19], (DEPTH, CONV_W, 2 * D_FF), 0.3),
        'conv_b': nrm(ks[20], (DEPTH, 2 * D_FF), 0.02),
        'w_down': nrm(ks[21], (DEPTH, D_FF, D), D_FF ** -0.5),
    }


def reference(x, c, ctx, c_ctx, w_mod, b_mod, norm1_g, w_in, na_q_g, na_k_g, na_rpb,
              diff_q_g, diff_k_g, diff_lambda, diff_subln_g, w_fourier, w_out, norm2_g,
              w_up, conv_w, conv_b, w_down):
    B, S, D = x.shape
    n_rows = S // GRID_W
    pos = jnp.arange(S)
    rows, cols = pos // GRID_W, pos % GRID_W
    cx = ctx
    for l in range(DEPTH):
        lam_init = 0.8 - 0.6 * math.exp(-0.3 * l)
        lq1, lk1, lq2, lk2 = (diff_lambda[l, i].astype(jnp.float32) for i in range(4))
        lam = jnp.exp(jnp.sum(lq1 * lk1)) - jnp.exp(jnp.sum(lq2 * lk2)) + lam_init

        mod_x = jax.nn.silu(c) @ w_mod[l] + b_mod[l]
        mod_c = jax.nn.silu(c_ctx) @ w_mod[l] + b_mod[l]
        sh1, sc1, g1, sh2, sc2, g2 = jnp.split(mod_x[:, None, :], 6, axis=-1)
        csh1, csc1, cg1, csh2, csc2, cg2 = jnp.split(mod_c, 6)

        hx = _modulate(_rmsnorm(x, norm1_g[l]), sh1, sc1)
        hc = _modulate(_rmsnorm(cx, norm1_g[l]), csh1, csc1)
        nqx, nkx, nvx, dqx, dkx, dvx, fux = _split_heads(hx @ w_in[l], na_q_g[l], na_k_g[l], diff_q_g[l], diff_k_g[l])
        nqc, nkc, nvc, dqc, dkc, dvc, fuc = _split_heads(hc @ w_in[l], na_q_g[l], na_k_g[l], diff_q_g[l], diff_k_g[l])

        na_x = _neighbourhood_attention(nqx, nkx, nvx, nkc, nvc, na_rpb[l], n_rows)
        q1x = _axial_rope(dqx[..., 0, :], rows, cols)
        q2x = _axial_rope(dqx[..., 1, :], rows, cols)
        k1x = _axial_rope(dkx[..., 0, :], rows, cols)
        k2x = _axial_rope(dkx[..., 1, :], rows, cols)
        diff_x = _diff_attention_latent(q1x, q2x, k1x, k2x, dvx, dkc[..., 0, :], dkc[..., 1, :], dvc, lam)
        x = x + g1 * _merge(na_x, diff_x, fux, diff_subln_g[l], lam_init, w_fourier[l], w_out[l])
        x = x + g2 * _conv_ffn(_modulate(_rmsnorm(x, norm2_g[l]), sh2, sc2), w_up[l], conv_w[l], conv_b[l], w_down[l])

        if l < DEPTH - 1:
            na_c = _dense_attention(nqc, nkc, nvc)
            diff_c = _diff_attend(dqc[..., 0, :], dqc[..., 1, :], dkc[..., 0, :], dkc[..., 1, :], dvc, lam)
            cx = cx + cg1 * _merge(na_c, diff_c, fuc, diff_subln_g[l], lam_init, w_fourier[l], w_out[l])
            cx = cx + cg2 * _conv_ffn(_modulate(_rmsnorm(cx, norm2_g[l]), csh2, csc2), w_up[l], conv_w[l], conv_b[l], w_down[l])
    return x
```

```python
import math
from contextlib import ExitStack

import numpy as np
import ml_dtypes

import concourse.bass as bass
import concourse.mybir as mybir
from concourse.bass_utils import run_bass_kernel_spmd

LAST_DINS = []
F32 = mybir.dt.float32
BF16 = mybir.dt.bfloat16
AF = mybir.ActivationFunctionType
ALU = mybir.AluOpType
AX = mybir.AxisListType
NPBF = ml_dtypes.bfloat16

D = 1024
S = 8192
NCTX = 256
SE = S + NCTX
DEPTH = 2
DFF = 2816
EPS = 1e-6
NCORES = 8
HALO = 2
LATW = 2048 + 2 * HALO
PAD0 = LATW
CTX0 = LATW + 1
W = CTX0 + NCTX + 1
DISJ = [(0, 512), (512, 1024), (1024, 1536), (1536, 2048), (2048, W)]
FFNB = [(1, 511), (511, 1021), (1021, 1531), (1531, 2041), (2041, W - 1)]
NEG = -30000.0


def segs(c0, c1):
    out = []
    if c0 < CTX0:
        out.append((c0, min(c1, CTX0), 0))
    if c1 > CTX0:
        out.append((max(c0, CTX0), c1, 1))
    return out


class Res:
    __slots__ = ("name", "last_w", "rd", "rd_dma", "sem", "ndma", "base", "last_use", "retired")

    def __init__(self, name):
        self.name = name
        self.last_w = None
        self.rd = {}
        self.rd_dma = []
        self.sem = None
        self.ndma = 0
        self.base = 0
        self.last_use = -1
        self.retired = False


class _Op:
    __slots__ = ("eng", "fn", "deps", "dres", "ms", "k", "inc")


class Sched:
    def __init__(self, nc, stack):
        self.nc = nc
        self.stack = stack
        self.ops = []
        self.eo = dict(pe=nc.tensor, act=nc.scalar, dve=nc.vector, pool=nc.gpsimd, sp=nc.sync)
        self.stores = []
        self.frontier = set()
        self.bars = []
        self.last_eng = {}
        self.dma_last = {}

    def barrier(self):
        self.frontier = set(self.last_eng.values()) | set(self.dma_last.values())
        self.bars.append(len(self.ops))

    def _deps(self, eng, r, w, acc):
        deps = set(self.frontier)
        for x in r:
            if x.last_w is not None:
                deps.add(x.last_w)
        for x in w:
            if x.last_w is not None:
                lw = self.ops[x.last_w]
                if not (acc and eng == "pe" and lw.eng == "pe" and lw.dres is None):
                    deps.add(x.last_w)
            deps.update(x.rd.values())
            deps.update(x.rd_dma)
        return deps

    def op(self, eng, fn, r=(), w=(), acc=False):
        o = _Op()
        o.eng, o.fn, o.dres, o.ms, o.k = eng, fn, None, False, 0
        o.deps = self._deps(eng, r, w, acc)
        idx = len(self.ops)
        self.ops.append(o)
        self.last_eng[eng] = idx
        for x in r:
            x.rd[eng] = idx
            x.last_use = idx
        for x in w:
            x.last_use = idx
        for x in w:
            x.last_w = idx
            x.rd = {}
            x.rd_dma = []
        return idx

    def dma(self, q, fn, r=(), w=None, store=False, inc=16):
        o = _Op()
        o.eng, o.fn, o.dres, o.ms = q, fn, w, False
        o.inc = inc
        o.deps = self._deps(q, r, (w,), False)
        idx = len(self.ops)
        self.ops.append(o)
        w.ndma += inc
        o.k = w.ndma
        self.dma_last[id(w)] = idx
        w.last_use = idx
        for x in r:
            x.rd_dma.append(idx)
            x.last_use = idx
        w.last_w = idx
        w.rd = {}
        w.rd_dma = []
        if store:
            self.stores.append(idx)
        return idx

    def emit(self):
        nc, ops = self.nc, self.ops
        if getattr(self, "pre_emit", None) is not None:
            self.pre_emit()
        fin = _Op()
        fin.eng, fin.fn, fin.dres, fin.ms, fin.k = "sp", None, None, False, 0
        fin.deps = set(self.stores)
        ops.append(fin)
        for o in ops:
            for d in o.deps:
                if ops[d].dres is None:
                    ops[d].ms = True
        esem = {}
        cnt = {}
        for e in self.eo:
            esem[e] = self.stack.enter_context(nc.semaphore("s_" + e))
            cnt[e] = 0
        waited = {e: {} for e in self.eo}
        nsem = len(esem)
        free_sems = []
        sem_res = []
        bars = set(self.bars)
        for oi, o in enumerate(ops):
            if oi in bars:
                for rs_ in sem_res:
                    if not rs_.retired and rs_.last_use < oi:
                        rs_.retired = True
                        free_sems.append((rs_.sem, rs_.base + rs_.ndma))
            ws = {}
            for d in o.deps:
                dop = ops[d]
                if dop.dres is None:
                    key, sem, val = dop.eng, esem[dop.eng], dop.k
                else:
                    rs = dop.dres
                    key, sem, val = id(rs), rs.sem, rs.base + dop.k
                if waited[o.eng].get(key, 0) >= val:
                    continue
                if key not in ws or ws[key][1] < val:
                    ws[key] = (sem, val)
            eobj = self.eo[o.eng]
            for key, (sem, val) in ws.items():
                eobj.wait_ge(sem, val)
                waited[o.eng][key] = val
            if o.fn is None:
                continue
            ins = o.fn()
            if o.dres is not None:
                rs = o.dres
                if rs.sem is None:
                    if free_sems and o.inc == 16:
                        rs.sem, rs.base = free_sems.pop()
                    else:
                        rs.sem = self.stack.enter_context(nc.semaphore("d%d" % nsem))
                        nsem += 1
                    if o.inc == 16:
                        sem_res.append(rs)
                ins.then_inc(rs.sem, o.inc)
            elif o.ms:
                cnt[o.eng] += 1
                o.k = cnt[o.eng]
                ins.then_inc(esem[o.eng], 1)
        self.nsem = nsem
        assert nsem < 240, nsem


class Ctx:
    def __init__(self):
        self.nc = bass.Bass("TRN2", target_bir_lowering=False)
        self.stack = ExitStack()
        self.S = Sched(self.nc, self.stack)
        nc = self.nc
        self.ps_all = nc.alloc_psum_tensor("ps_all", [128, 4096], F32)
        self.ps = [self.ps_all[:, 512 * i:512 * (i + 1)] for i in range(8)]
        self.psr = [Res("psb%d" % i) for i in range(8)]
        self.nres = 0
        self.scope = self.stack
        self.dins = {}
        self.stoks = [Res("stok%d" % i) for i in range(6)]
        self.nst = 0

    def stok(self):
        self.nst += 1
        return self.stoks[self.nst % len(self.stoks)]

    def begin_phase(self):
        self.S.barrier()
        self.scope = ExitStack()

    def end_phase(self):
        self.scope.close()
        self.scope = self.stack

    def uname(self, name):
        self.nres += 1
        return "%s_%d" % (name, self.nres)

    def dram(self, name, shape, dt):
        return self.nc.dram_tensor(name, list(shape), dt)

    def res(self, name=None):
        self.nres += 1
        return Res(name or ("r%d" % self.nres))

    def din(self, name, shape, dt):
        if name not in self.dins:
            self.dins[name] = self.nc.dram_tensor(name, list(shape), dt, kind="ExternalInput").ap()
        return self.dins[name]

    def dout(self, name, shape, dt):
        return self.nc.dram_tensor(name, list(shape), dt, kind="ExternalOutput").ap()

    def sb(self, name, shape, dt):
        self.nres += 1
        return self.scope.enter_context(self.nc.sbuf_tensor("%s_%d" % (name, self.nres), list(shape), dt))

    def mm(self, out, lhsT, rhs, start, stop, r, w):
        nc = self.nc
        return self.S.op("pe", lambda: nc.tensor.matmul(out, lhsT=lhsT, rhs=rhs, start=start, stop=stop), r=r, w=w, acc=True)

    def act(self, out, in_, func, r, w, scale=1.0, bias=0.0):
        nc = self.nc
        return self.S.op("act", lambda: nc.scalar.activation(out=out, in_=in_, func=func, bias=bias, scale=scale), r=r, w=w)

    def tt(self, out, in0, in1, op, r, w, eng="dve"):
        e = self.nc.vector if eng == "dve" else self.nc.gpsimd
        return self.S.op(eng, lambda: e.tensor_tensor(out=out, in0=in0, in1=in1, op=op), r=r, w=w)

    def stt(self, out, in0, scalar, in1, op0, op1, r, w, eng="dve"):
        e = self.nc.vector if eng == "dve" else self.nc.gpsimd
        return self.S.op(eng, lambda: e.scalar_tensor_tensor(out=out, in0=in0, scalar=scalar, in1=in1, op0=op0, op1=op1), r=r, w=w)

    def ts(self, out, in0, s1, s2, op0, op1, r, w, eng="dve"):
        e = self.nc.vector if eng == "dve" else self.nc.gpsimd
        if s2 is None:
            return self.S.op(eng, lambda: e.tensor_scalar(out=out, in0=in0, scalar1=s1, scalar2=None, op0=op0), r=r, w=w)
        return self.S.op(eng, lambda: e.tensor_scalar(out=out, in0=in0, scalar1=s1, scalar2=s2, op0=op0, op1=op1), r=r, w=w)

    def recip(self, out, in_, r, w):
        nc = self.nc
        return self.S.op("dve", lambda: nc.vector.reciprocal(out=out, in_=in_), r=r, w=w)

    def copy(self, out, in_, r, w, eng="dve"):
        if eng == "act":
            return self.act(out, in_, AF.Copy, r, w)
        e = self.nc.vector if eng == "dve" else self.nc.gpsimd
        return self.S.op(eng, lambda: e.tensor_copy(out=out, in_=in_), r=r, w=w)

    def memset(self, ap, val, w, eng="dve"):
        e = self.nc.vector if eng == "dve" else self.nc.gpsimd
        return self.S.op(eng, lambda: e.memset(ap, val), r=(), w=w)

    def load(self, out, in_, w, q="sp", r=()):
        e = self.S.eo[q]
        return self.S.dma(q, lambda: e.dma_start(out=out, in_=in_), r=r, w=w)

    def store(self, out, in_, r, w, q="sp"):
        e = self.S.eo[q]
        return self.S.dma(q, lambda: e.dma_start(out=out, in_=in_), r=r, w=w, store=True)

    def finish(self):
        global LAST_DINS
        LAST_DINS = list(self.dins.keys())
        self.S.emit()
        self.stack.close()
        return self.nc


def bc_mid(ap2d, k):
    return ap2d.unsqueeze(1).to_broadcast([ap2d.shape[0], k, ap2d.shape[1]])


def load_const(cx, name, dram_ap, shape, dt=F32, q="sp"):
    t = cx.sb(name, shape, dt)
    r = cx.res(name)
    cx.load(t[:], dram_ap, w=r, q=q)
    return t, r


def norm_mod_block(cx, xs, xs_r, c0, c1, Avec, Svec, vec_r, ones_t, ones_r, scr, out_t, out_r, out_c0, mask=None):
    n = c1 - c0
    sq, sq_r, tb, tb_r, rs, rs_r, pb = scr["sq"], scr["sq_r"], scr["tb"], scr["tb_r"], scr["rs"], scr["rs_r"], scr["pb"]
    kh = scr.get("kh", 8)
    ps, ps_r = cx.ps[pb], cx.psr[pb]
    for h0 in range(0, 8, kh):
        cx.act(sq[:, 0:kh, 0:n], xs[:, h0:h0 + kh, c0:c1], AF.Square, r=[xs_r], w=[sq_r])
        for kc in range(kh):
            cx.mm(ps[:, 0:n], ones_t[:], sq[:, kc, 0:n], h0 + kc == 0, h0 + kc == 7, r=[ones_r, sq_r], w=[ps_r])
    cx.act(rs[:, 0:n], ps[:, 0:n], AF.Sqrt, r=[ps_r], w=[rs_r], scale=1.0 / D, bias=EPS)
    cx.recip(rs[:, 0:n], rs[:, 0:n], r=[rs_r], w=[rs_r])
    for h0 in range(0, 8, kh):
        cx.tt(tb[:, 0:kh, 0:n], xs[:, h0:h0 + kh, c0:c1], bc_mid(rs[:, 0:n], kh), ALU.mult, r=[xs_r, rs_r], w=[tb_r])
        for (a, b, j) in segs(c0, c1):
            for kc in range(kh):
                cx.act(out_t[:, h0 + kc, out_c0 + a - c0:out_c0 + b - c0], tb[:, kc, a - c0:b - c0], AF.Identity,
                       r=[tb_r, vec_r], w=[out_r], scale=Avec[:, h0 + kc, j:j + 1], bias=Svec[:, h0 + kc, j:j + 1])
    if mask is not None:
        mt, mr = mask
        cx.tt(out_t[:, :, out_c0:out_c0 + n], out_t[:, :, out_c0:out_c0 + n], bc_mid(mt[:, c0:c1], 8), ALU.mult,
              r=[out_r, mr], w=[out_r], eng="pool")


def make_AS(cx, name, modt, mod_r, g_t, g_r, sc_off, sh_off):
    A = cx.sb(name + "_A", [128, 8, 2], F32)
    r = cx.res(name)
    cx.ts(A[:], modt[:, sc_off:sc_off + 8, :], 1.0, None, ALU.add, None, r=[mod_r], w=[r])
    cx.tt(A[:], A[:], g_t[:, :].unsqueeze(2).to_broadcast([128, 8, 2]), ALU.mult, r=[r, g_r], w=[r])
    return A, modt[:, sh_off:sh_off + 8, :], r


def build_prep(cx=None, io=None):
    own = cx is None
    if own:
        cx = Ctx()
    nc = cx.nc
    xT = cx.din("xT", [D, W], F32)
    cvec = cx.din("cvec", [128, 8, 2], F32)
    wmod = cx.din("w_mod", [DEPTH, D, 6 * D], F32)
    bmod = cx.din("b_mod", [DEPTH, 128, 48], F32)
    g1 = cx.din("g1", [128, 8], F32)
    ones_d = cx.din("ones", [128, 128], F32)
    if own:
        modT = cx.dout("modT", [DEPTH, 128, 48, 2], F32)
        hT = cx.dout("hT", [D, W], BF16)
        modsb = cx.sb("modsb", [128, DEPTH, 48, 2], F32)
        mod_r = cx.res("modsb")
    else:
        modsb, mod_r = io["modsb"], io["mod_r"]

    xs = cx.sb("xs", [128, 8, W], F32)
    xs_r = cx.res("xs")
    xv = xT.rearrange("(kc p) n -> p kc n", p=128)
    cx.load(xs[:], xv, w=xs_r)
    ones_t, ones_r = load_const(cx, "ones_t", ones_d, [128, 128])
    g1_t, g1_r = load_const(cx, "g1_t", g1, [128, 8])
    cv, cv_r = load_const(cx, "cv", cvec, [128, 8, 2])
    bm, bm_r = load_const(cx, "bm", bmod.rearrange("l p o -> p l o"), [128, DEPTH, 48])
    sc = cx.sb("silu_c", [128, 8, 2], F32)
    sc_r = cx.res("silu_c")
    cx.act(sc[:], cv[:], AF.Silu, r=[cv_r], w=[sc_r])
    def emit_hT():
        scr = dict(sq=cx.sb("sq", [128, 8, 512], F32), sq_r=cx.res(), tb=cx.sb("tb", [128, 8, 512], F32), tb_r=cx.res(),
                   rs=cx.sb("rs", [128, 512], F32), rs_r=cx.res(), pb=2)
        A, Sv, vr = make_AS(cx, "as1", modsb[:, 0], mod_r, g1_t, g1_r, 8, 0)
        hb = [cx.sb("hb%d" % i, [128, 8, 512], BF16) for i in range(2)]
        hb_r = [cx.res() for i in range(2)]
        ho_r = [cx.res() for i in range(2)]
        if own:
            hv = hT.rearrange("(kc p) n -> p kc n", p=128)
        for bi, (c0, c1) in enumerate(DISJ):
            i = bi % 2
            scr["pb"] = 2 + (bi % 2)
            norm_mod_block(cx, xs, xs_r, c0, c1, A, Sv, vr, ones_t, ones_r, scr, hb[i], hb_r[i], 0)
            if own:
                cx.store(hv[:, :, c0:c1], hb[i][:, :, 0:c1 - c0], r=[hb_r[i]], w=ho_r[i])
            else:
                io["h_store"](hb[i], hb_r[i], c0, c1, c0)
    NB = 4
    wb = [cx.sb("wmb%d" % i, [128, 8, 512], F32) for i in range(NB)]
    wb_r = [cx.res("wmb%d" % i) for i in range(NB)]
    it = 0
    for l in range(DEPTH):
        for og in range(12):
            i = it % NB
            cx.load(wb[i][:], wmod[l, :, og * 512:(og + 1) * 512].rearrange("(kc p) n -> p kc n", p=128), w=wb_r[i],
                    q=("sp", "act", "pool")[it % 3])
            pb = it % 2
            for o4 in range(4):
                for kc in range(8):
                    cx.mm(cx.ps[pb][:, o4 * 2:o4 * 2 + 2], wb[i][:, kc, o4 * 128:(o4 + 1) * 128], sc[:, kc, :], kc == 0, kc == 7,
                          r=[wb_r[i], sc_r], w=[cx.psr[pb]])
            for o4 in range(4):
                oc = og * 4 + o4
                cx.act(modsb[:, l, oc, :], cx.ps[pb][:, o4 * 2:o4 * 2 + 2], AF.Identity, r=[cx.psr[pb], bm_r], w=[mod_r],
                       bias=bm[:, l, oc:oc + 1])
            it += 1
            if l == 0 and og == 3:
                emit_hT()
    if own:
        mo_r = cx.res("modT_out")
        cx.store(modT.rearrange("l p o j -> p l o j"), modsb[:], r=[mod_r], w=mo_r)
    return cx.finish() if own else None


def pvec(v, nch):
    return np.ascontiguousarray(v.reshape(nch, 128).T)


def local_xT(x_b, ctx_b, j):
    out = np.zeros((W, D), np.float32)
    lo = 2048 * j - HALO
    a, b_ = max(lo, 0), min(lo + LATW, S)
    out[a - lo:b_ - lo] = x_b[a:b_]
    out[CTX0:CTX0 + NCTX] = ctx_b
    return np.ascontiguousarray(out.T)


def run_prog(nc, in_maps):
    res = run_bass_kernel_spmd(nc, in_maps, core_ids=list(range(len(in_maps))))
    return res.results


def prep_inputs(inp, b, j):
    cv = np.stack([inp["c"][b], inp["c_ctx"]], axis=-1)
    return {
        "xT": local_xT(inp["x"][b], inp["ctx"][b], j),
        "cvec": np.ascontiguousarray(cv.reshape(8, 128, 2).transpose(1, 0, 2)),
        "w_mod": inp["w_mod"],
        "b_mod": np.ascontiguousarray(inp["b_mod"].reshape(DEPTH, 48, 128).transpose(0, 2, 1)),
        "g1": pvec(inp["norm1_g"][0], 8),
        "ones": np.ones((128, 128), np.float32),
    }


def build_post(last, cx=None, io=None, sfx="", sfxn=""):
    own = cx is None
    if own:
        cx = Ctx()
    nc = cx.nc
    g2d = cx.din("g2" + sfx, [128, 8], F32)
    ones_d = cx.din("ones", [128, 128], F32)
    w_out = cx.din("w_out" + sfx, [D, D], F32)
    w_four = cx.din("w_four" + sfx, [256, 256], F32)
    w_up = cx.din("w_up" + sfx, [D, 2 * DFF], F32)
    cwd = cx.din("cw" + sfx, [128, 44, 3], F32)
    cbd = cx.din("cb" + sfx, [128, 44], F32)
    w_down = cx.din("w_down" + sfx, [DFF, D], F32)
    maskd = cx.din("mask", [128, W], BF16)
    if own:
        xT = cx.din("xT", [D, W], F32)
        yT = cx.din("yT", [D, W], BF16)
        modT = cx.din("modT", [128, 48, 2], F32)
        xo = cx.dout("xTo", [D, W], F32)
        x_src, x_src_r = xT.rearrange("(kc p) n -> p kc n", p=128), []
        x_dst = xo.rearrange("(kc p) n -> p kc n", p=128)
        if not last:
            modTn = cx.din("modTn", [128, 48, 2], F32)
            hTo = cx.dout("hTo", [D, W], BF16)
    else:
        x_src, x_src_r, x_dst = io["x_src"], io["x_src_r"], io["x_dst"]
    if not last:
        g1nd = cx.din("g1n" + sfxn, [128, 8], F32)

    xs = cx.sb("xs", [128, 8, W], F32)
    xs_r = cx.res("xs")
    cx.load(xs[:], x_src, w=xs_r, r=x_src_r)
    h2 = cx.sb("h2", [128, 8, W], BF16)
    h2_r = [cx.res("h2_%d" % i) for i in range(len(DISJ))]
    ones_t, ones_r = load_const(cx, "ones_t", ones_d, [128, 128])
    if own:
        mod_t, mod_r = load_const(cx, "mod_t", modT, [128, 48, 2])
    else:
        mod_t, mod_r = io["mod_t"], io["mod_r"]
    g2_t, g2_r = load_const(cx, "g2_t", g2d, [128, 8])
    cw_t, cw_r = load_const(cx, "cw_t", cwd, [128, 44, 3])
    cb_t, cb_r = load_const(cx, "cb_t", cbd, [128, 44])
    mk_t, mk_r = load_const(cx, "mk_t", maskd, [128, W], BF16)
    A2, S2, v2r = make_AS(cx, "as2", mod_t, mod_r, g2_t, g2_r, 32, 24)
    G1 = mod_t[:, 16:24, :]
    G2 = mod_t[:, 40:48, :]
    if own:
        yv = yT.rearrange("(kc p) n -> p kc n", p=128)

        def y_load(t_, parts, c0, c1):
            cx.load(t_[:, :, 0:c1 - c0], yv[:, :, c0:c1], w=parts[0])
    else:
        y_load = io["y_load"]

    with ExitStack() as st1:
        def sb1(name, shape, dt):
            return st1.enter_context(nc.sbuf_tensor(cx.uname(name), list(shape), dt))
        wo = sb1("wo", [128, 8, D], BF16)
        wo_r = cx.res("wo")
        cx.load(wo[:], w_out.rearrange("(kc p) n -> p kc n", p=128), w=wo_r, q="pool")
        wf = sb1("wf", [128, 2, 256], BF16)
        wf_r = cx.res("wf")
        cx.load(wf[:], w_four.rearrange("(kc p) n -> p kc n", p=128), w=wf_r, q="pool")
        ysb = [sb1("ysb%d" % i, [128, 8, 512], BF16) for i in range(2)]
        ys_r = [[cx.res() for _ in range(26)] for i in range(2)]
        fo = [sb1("fo%d" % i, [128, 2, 512], BF16) for i in range(2)]
        fo_r = [cx.res() for i in range(2)]
        scr = dict(sq=sb1("sq", [128, 8, 512], F32), sq_r=cx.res(), tb=sb1("tb", [128, 8, 512], F32), tb_r=cx.res(),
                   rs=sb1("rs", [128, 512], F32), rs_r=cx.res(), pb=7)
        pbi = 0
        deferred = [None]
        for bi, (c0, c1) in enumerate(DISJ):
            n = c1 - c0
            i = bi % 2
            y_load(ysb[i], ys_r[i], c0, c1)
            for oc in range(2):
                pb = pbi % 6
                pbi += 1
                for kc in range(2):
                    cx.mm(cx.ps[pb][:, 0:n], wf[:, kc, oc * 128:(oc + 1) * 128], ysb[i][:, 6 + kc, 0:n], kc == 0, kc == 1,
                          r=[wf_r] + ys_r[i], w=[cx.psr[pb]])
                cx.act(fo[i][:, oc, 0:n], cx.ps[pb][:, 0:n], AF.Copy, r=[cx.psr[pb]], w=[fo_r[i]])
            for dc in range(8):
                pb = pbi % 6
                pbi += 1
                for kc in range(8):
                    rhs = ysb[i][:, kc, 0:n] if kc < 6 else fo[i][:, kc - 6, 0:n]
                    cx.mm(cx.ps[pb][:, 0:n], wo[:, kc, dc * 128:(dc + 1) * 128], rhs, kc == 0, kc == 7,
                          r=[wo_r, fo_r[i]] + ys_r[i], w=[cx.psr[pb]])
                for (a, b, j) in segs(c0, c1):
                    cx.stt(xs[:, dc, a:b], cx.ps[pb][:, a - c0:b - c0], G1[:, dc, j:j + 1], xs[:, dc, a:b], ALU.mult, ALU.add,
                           r=[cx.psr[pb], mod_r, xs_r], w=[xs_r])
            if deferred[0] is not None:
                deferred[0]()

            def do_norm(bi=bi, c0=c0, c1=c1):
                scr["pb"] = 6 + (bi % 2)
                norm_mod_block(cx, xs, xs_r, c0, c1, A2, S2, v2r, ones_t, ones_r, scr, h2, h2_r[bi], c0, mask=(mk_t, mk_r))
            deferred[0] = do_norm
        deferred[0]()

    cx.S.barrier()
    if not last:
        if own:
            modn_t, modn_r = load_const(cx, "modn_t", modTn, [128, 48, 2])
            hv_o = hTo.rearrange("(kc p) n -> p kc n", p=128)
        else:
            modn_t, modn_r = io["modn_t"], io["mod_r"]
        g1n_t, g1n_r = load_const(cx, "g1n_t", g1nd, [128, 8])
        A1n, S1n, v1nr = make_AS(cx, "as1n", modn_t, modn_r, g1n_t, g1n_r, 8, 0)
    with ExitStack() as st2:
        def sb2(name, shape, dt):
            return st2.enter_context(nc.sbuf_tensor(cx.uname(name), list(shape), dt))
        NWB = 4
        NWD = 6
        wg = [sb2("wg%d" % i, [128, 8, 128], BF16) for i in range(NWB)]
        wv = [sb2("wv%d" % i, [128, 8, 128], BF16) for i in range(NWB)]
        wd = [sb2("wd%d" % i, [128, D], BF16) for i in range(NWD)]
        wg_r = [cx.res() for i in range(NWB)]
        wv_r = [cx.res() for i in range(NWB)]
        wd_r = [cx.res() for i in range(NWD)]
        aT = sb2("aT", [128, 22, 512], BF16)
        aT_rs = [cx.res("aT%d" % i) for i in range(22)]
        accg = [sb2("accg%d" % i, [128, 512], F32) for i in range(2)]
        accv = [sb2("accv%d" % i, [128, 512], F32) for i in range(2)]
        sg = [sb2("sg%d" % i, [128, 512], F32) for i in range(2)]
        ag_r = [cx.res() for i in range(2)]
        av_r = [cx.res() for i in range(2)]
        sg_r = [cx.res() for i in range(2)]
        wupv = w_up.rearrange("(kc p) n -> p kc n", p=128)
        it = 0
        if not last:
            scr3 = dict(sq=sb2("sq3h", [128, 4, 512], F32), sq_r=cx.res(), tb=sb2("tb3h", [128, 4, 512], F32), tb_r=cx.res(),
                        rs=sb2("rs3h", [128, 512], F32), rs_r=cx.res(), pb=7, kh=4)
            hst = sb2("hst", [128, 8, 512], BF16)
            hst_r = cx.res("hst")
        for bi, (c0, c1) in enumerate(FFNB):
            n = c1 - c0
            un = n + 2
            hres = [h2_r[k] for k, (d0, d1) in enumerate(DISJ) if d0 < c1 + 1 and d1 > c0 - 1]
            for c in range(NWD):
                cx.load(wd[c][:], w_down[c * 128:(c + 1) * 128, :], w=wd_r[c], q="pool")
            for c in range(22):
                i = it % NWB
                i2 = it % 2
                it += 1
                cx.load(wg[i][:], wupv[:, :, c * 128:(c + 1) * 128], w=wg_r[i], q="pool")
                cx.load(wv[i][:], wupv[:, :, DFF + c * 128:DFF + (c + 1) * 128], w=wv_r[i], q="pool")
                pg, pv = 2 * i2, 2 * i2 + 1
                for kc in range(8):
                    cx.mm(cx.ps[pg][:, 0:un], wg[i][:, kc, :], h2[:, kc, c0 - 1:c1 + 1], kc == 0, kc == 7,
                          r=[wg_r[i]] + hres, w=[cx.psr[pg]])
                for kc in range(8):
                    cx.mm(cx.ps[pv][:, 0:un], wv[i][:, kc, :], h2[:, kc, c0 - 1:c1 + 1], kc == 0, kc == 7,
                          r=[wv_r[i]] + hres, w=[cx.psr[pv]])
                for (pp, acc, acc_r, ch) in ((pg, accg[i2], ag_r[i2], c), (pv, accv[i2], av_r[i2], c + 22)):
                    cx.act(acc[:, 0:n], cx.ps[pp][:, 1:1 + n], AF.Identity, r=[cx.psr[pp], cw_r, cb_r], w=[acc_r],
                           scale=cw_t[:, ch, 1:2], bias=cb_t[:, ch:ch + 1])
                    cx.stt(acc[:, 0:n], cx.ps[pp][:, 0:n], cw_t[:, ch, 0:1], acc[:, 0:n], ALU.mult, ALU.add,
                           r=[cx.psr[pp], cw_r, acc_r], w=[acc_r])
                    cx.stt(acc[:, 0:n], cx.ps[pp][:, 2:2 + n], cw_t[:, ch, 2:3], acc[:, 0:n], ALU.mult, ALU.add,
                           r=[cx.psr[pp], cw_r, acc_r], w=[acc_r])
                cx.act(sg[i2][:, 0:n], accg[i2][:, 0:n], AF.Silu, r=[ag_r[i2]], w=[sg_r[i2]])
                cx.tt(aT[:, c, 0:n], sg[i2][:, 0:n], accv[i2][:, 0:n], ALU.mult, r=[sg_r[i2], av_r[i2]], w=[aT_rs[c]])
            for c in range(22):
                i = c % NWD
                for dc in range(8):
                    cx.mm(cx.ps[dc][:, 0:n], wd[i][:, dc * 128:(dc + 1) * 128], aT[:, c, 0:n], c == 0, c == 21,
                          r=[wd_r[i], aT_rs[c]], w=[cx.psr[dc]])
                if c + NWD < 22:
                    cx.load(wd[i][:], w_down[(c + NWD) * 128:(c + NWD + 1) * 128, :], w=wd_r[i], q="pool")
            for dc in range(8):
                for (a, b, j) in segs(c0, c1):
                    cx.stt(xs[:, dc, a:b], cx.ps[dc][:, a - c0:b - c0], G2[:, dc, j:j + 1], xs[:, dc, a:b], ALU.mult, ALU.add,
                           r=[cx.psr[dc], mod_r, xs_r], w=[xs_r])
            if not last:
                norm_mod_block(cx, xs, xs_r, c0, c1, A1n, S1n, v1nr, ones_t, ones_r, scr3, hst, hst_r, 0)
                if own:
                    cx.store(hv_o[:, :, c0:c1], hst[:, :, 0:n], r=[hst_r], w=cx.stok())
                else:
                    io["h_store"](hst, hst_r, c0, c1, c0)

    cx.S.barrier()
    xo_r = io["x_dst_r"] if (not own and "x_dst_r" in io) else cx.res("xo")
    cx.store(x_dst, xs[:], r=[xs_r], w=xo_r)
    return cx.finish() if own else None


def post_inputs(inp, l, xT_loc, yT_loc, modT_l, modT_n, mask, last):
    d = {
        "xT": xT_loc, "yT": yT_loc, "modT": modT_l,
        "g2": pvec(inp["norm2_g"][l], 8), "ones": np.ones((128, 128), np.float32),
        "w_out": inp["w_out"][l], "w_four": inp["w_fourier"][l], "w_up": inp["w_up"][l],
        "cw": np.ascontiguousarray(inp["conv_w"][l].reshape(3, 44, 128).transpose(2, 1, 0)),
        "cb": pvec(inp["conv_b"][l], 44), "w_down": inp["w_down"][l], "mask": mask,
    }
    if not last:
        d["modTn"] = modT_n
        d["g1n"] = pvec(inp["norm1_g"][l + 1], 8)
    return d


def col_mask(j):
    m = np.zeros((W,), np.float32)
    lo = 2048 * j - HALO
    a, b_ = max(lo, 0), min(lo + LATW, S)
    m[a - lo:b_ - lo] = 1.0
    m[CTX0:CTX0 + NCTX] = 1.0
    return np.ascontiguousarray(np.broadcast_to(m[None, :], (128, W))).astype(NPBF)


def local_cols(full_T, j):
    R = full_T.shape[0]
    out = np.zeros((R, W), full_T.dtype)
    lo = 2048 * j - HALO
    a, b_ = max(lo, 0), min(lo + LATW, S)
    out[:, a - lo:b_ - lo] = full_T[:, a:b_]
    out[:, CTX0:CTX0 + NCTX] = full_T[:, S:SE]
    return out


TB = [(i * 512, 512) for i in range(16)] + [(S, NCTX)]
NKT = SE // 128


def na_tiles(qb):
    if qb == 0:
        return [(kt, 0, kt) for kt in range(0, 6)]
    if qb == 15:
        return [(kt, 2, kt - 58) for kt in range(58, 64)]
    return [(kt, 1, kt - (4 * qb - 2)) for kt in range(4 * qb - 2, 4 * qb + 6)]


def build_mixer(ctxq, lam_init, debug=False, stop_after=99, cx=None, io=None, sfx=""):
    own = cx is None
    if own:
        cx = Ctx()
    nc = cx.nc
    w_in = cx.din("w_in" + sfx, [D, 640], F32)
    gains = cx.din("gains" + sfx, [128, 4], F32)
    cosd = cx.din("cos", [128, SE], F32)
    sind = cx.din("sin", [128, SE], F32)
    rperm = cx.din("rperm", [128, 128], F32)
    bo64 = cx.din("bo64", [128, 128], F32)
    bo32 = cx.din("bo32", [128, 128], F32)
    lamd = cx.din("lam" + sfx, [64, 4, 32], F32)
    biasd = cx.din("bias" + sfx, [2, 3, 8, 128, 512], F32)
    ones_bf = cx.din("ones_bf", [128, 128], BF16)
    fa_d = cx.din("f_rhs_a", [64, 128], BF16)
    fac_d = cx.din("f_rhs_ac", [64, 128], BF16)
    cs128_d = cx.din("f_cs128", [128, 256], F32)
    tab_d = cx.din("f_tab", [128, 2, 128], F32)
    c64s_d = cx.din("f_cs64s", [128, 2, 64], F32)
    c256_d = cx.din("f_cs256", [128, 2, 2, 256], F32)
    if own:
        hT = cx.din("hT", [D, SE], BF16)
        yT = cx.dout("yT", [256, SE], BF16)
        hv = hT.rearrange("(kc p) n -> p kc n", p=128)

        def h_load(t_, t_r, bi):
            t0_, n_ = TB[bi]
            cx.load(t_[:, :, 0:n_], hv[:, :, t0_:t0_ + n_], w=t_r)

        def y_store(r0, r1, t0_, n_, src, src_r, w):
            cx.store(yT[r0:r1, t0_:t0_ + n_], src, r=src_r, w=w)
    else:
        h_load, y_store = io["h_load"], io["y_store"]

    gn, gn_r = load_const(cx, "gn", gains, [128, 4])
    rp, rp_r = load_const(cx, "rp", rperm, [128, 128])
    b64, b64_r = load_const(cx, "b64", bo64, [128, 128])
    b32, b32_r = load_const(cx, "b32", bo32, [128, 128])
    lm, lm_r = load_const(cx, "lm", lamd, [64, 4, 32])
    onb, onb_r = load_const(cx, "onb", ones_bf, [128, 128], BF16)

    qna = cx.sb("qna", [128, SE], BF16)
    qna1 = cx.sb("qna1", [128, SE], BF16)
    qnz_r = cx.res("qnz")
    cx.memset(qna[64:128, :], 0.0, w=[qnz_r], eng="pool")
    cx.memset(qna1[0:64, :], 0.0, w=[qnz_r], eng="pool")
    kna = cx.sb("kna", [128, SE], BF16)
    vna = cx.sb("vna", [128, NKT, 2, 64], BF16)
    qd = cx.sb("qd", [128, SE], BF16)
    qd2 = cx.sb("qd2", [128, SE], BF16)
    kd = cx.sb("kd", [128, SE], BF16)
    qz_r = cx.res("qz")
    cx.memset(qd[:, :], 0.0, w=[qz_r], eng="pool")
    cx.memset(qd2[:, :], 0.0, w=[qz_r], eng="pool")
    cx.memset(kd[64:128, :], 0.0, w=[qz_r], eng="pool")
    vd = cx.sb("vd", [128, NKT, 128], BF16)
    st_fu = ExitStack()
    fuT = st_fu.enter_context(nc.sbuf_tensor(cx.uname("fuT"), [64, SE], BF16, side="right"))
    nb = len(TB)
    qna_r = [cx.res() for _ in range(nb)]
    kna_r = [cx.res() for _ in range(nb)]
    v_r = [cx.res() for _ in range(nb)]
    qd_r = [cx.res() for _ in range(nb)]
    kd_r = [cx.res() for _ in range(nb)]
    fu_r = [cx.res() for _ in range(nb)]
    vd1_r = cx.res("vd_ones")
    cx.memset(vd[:, :, 64:128], 1.0, w=[vd1_r], eng="pool")

    lt = cx.sb("lt", [64, 2, 32], F32)
    ls = cx.sb("ls", [64, 2], F32)
    nlam = cx.sb("nlam", [64, 1], F32)
    gsub = cx.sb("gsub", [64, 1], F32)
    lt_r, ls_r, nl_r, gs_r = cx.res(), cx.res(), cx.res(), cx.res()
    cx.tt(lt[:, 0, :], lm[:, 0, :], lm[:, 1, :], ALU.mult, r=[lm_r], w=[lt_r])
    cx.tt(lt[:, 1, :], lm[:, 2, :], lm[:, 3, :], ALU.mult, r=[lm_r, lt_r], w=[lt_r])
    cx.S.op("dve", lambda: nc.vector.reduce_sum(out=ls[:], in_=lt[:], axis=AX.X), r=[lt_r], w=[ls_r])
    cx.act(ls[:], ls[:], AF.Exp, r=[ls_r], w=[ls_r])
    cx.tt(nlam[:], ls[:, 1:2], ls[:, 0:1], ALU.subtract, r=[ls_r], w=[nl_r])
    cx.ts(nlam[:], nlam[:], -float(lam_init), None, ALU.add, None, r=[nl_r], w=[nl_r])
    cx.ts(gsub[:], gn[0:64, 3:4], 1.0 - float(lam_init), None, ALU.mult, None, r=[gn_r], w=[gs_r])

    with ExitStack() as st1:
        def sb1(name, shape, dt):
            return st1.enter_context(nc.sbuf_tensor(cx.uname(name), list(shape), dt))
        win = sb1("win", [128, 8, 640], BF16)
        win_r = cx.res("win")
        cx.load(win[:], w_in.rearrange("(kc p) n -> p kc n", p=128), w=win_r, q="pool")
        hb = [sb1("hb%d" % i, [128, 8, 512], BF16) for i in range(2)]
        hb_r = [cx.res() for _ in range(2)]
        cs = [sb1("cs%d" % i, [128, 2, 512], F32) for i in range(2)]
        cs_r = [cx.res() for _ in range(2)]
        xn = sb1("xn", [128, 512], F32)
        t1 = sb1("t1", [128, 512], F32)
        t2 = sb1("t2", [128, 512], F32)
        xn_r, t1_r, t2_r = cx.res(), cx.res(), cx.res()
        sq = [sb1("sq3_%d" % i, [128, 512], F32) for i in range(3)]
        rs = [sb1("rs3_%d" % i, [128, 512], F32) for i in range(3)]
        sq_r = [cx.res() for _ in range(3)]
        rs_r = [cx.res() for _ in range(3)]
        pending = [None]
        for bi, (t0, n) in enumerate(TB):
            i = bi % 2
            h_load(hb[i], hb_r[i], bi)
            cx.load(cs[i][:, 0, 0:n], cosd[:, t0:t0 + n], w=cs_r[i], q="act")
            cx.load(cs[i][:, 1, 0:n], sind[:, t0:t0 + n], w=cs_r[i], q="act")
            for ch in range(3):
                ps, ps_r = cx.ps[ch], cx.psr[ch]
                for kc in range(8):
                    cx.mm(ps[:, 0:n], win[:, kc, ch * 128:(ch + 1) * 128], hb[i][:, kc, 0:n], kc == 0, kc == 7,
                          r=[win_r, hb_r[i]], w=[ps_r])
                cx.act(sq[ch][:, 0:n], ps[:, 0:n], AF.Square, r=[ps_r], w=[sq_r[ch]])
            p4, p4_r = cx.ps[5], cx.psr[5]
            for kc in range(8):
                cx.mm(p4[0:64, 0:n], win[:, kc, 384:448], hb[i][:, kc, 0:n], kc == 0, kc == 7, r=[win_r, hb_r[i]], w=[p4_r])
            cx.act(fuT[:, t0:t0 + n], p4[0:64, 0:n], AF.Copy, r=[p4_r], w=[fu_r[bi]])
            for tt_ in range(n // 128):
                kt = t0 // 128 + tt_
                p5, p5_r = cx.ps[6 + tt_ % 2], cx.psr[6 + tt_ % 2]
                for kc in range(8):
                    cx.mm(p5[:, 0:192], hb[i][:, kc, tt_ * 128:(tt_ + 1) * 128], win[:, kc, 448:640], kc == 0, kc == 7,
                          r=[win_r, hb_r[i]], w=[p5_r])
                cx.copy(vna[:, kt, :, :], p5[:, 0:128].rearrange("p (h e) -> p h e", h=2), r=[p5_r], w=[v_r[bi]], eng="act")
                cx.copy(vd[:, kt, 0:64], p5[:, 128:192], r=[p5_r, vd1_r], w=[v_r[bi]], eng="dve")
            if pending[0] is not None:
                pending[0]()
            for ch in range(3):
                ps, ps_r = cx.ps[ch], cx.psr[ch]
                p2, p2_r = cx.ps[3 + ch % 2], cx.psr[3 + ch % 2]
                bo, bo_r = (b64, b64_r) if ch < 2 else (b32, b32_r)
                cx.mm(p2[:, 0:n], bo[:], sq[ch][:, 0:n], True, True, r=[bo_r, sq_r[ch]], w=[p2_r])
                cx.act(rs[ch][:, 0:n], p2[:, 0:n], AF.Sqrt, r=[p2_r], w=[rs_r[ch]], bias=EPS)
                cx.recip(rs[ch][:, 0:n], rs[ch][:, 0:n], r=[rs_r[ch]], w=[rs_r[ch]])
                if ch == 0:
                    cx.stt(qna[0:64, t0:t0 + n], ps[0:64, 0:n], gn[0:64, 0:1], rs[ch][0:64, 0:n], ALU.mult, ALU.mult,
                           r=[ps_r, gn_r, rs_r[ch], qnz_r], w=[qna_r[bi]])
                    cx.stt(qna1[64:128, t0:t0 + n], ps[64:128, 0:n], gn[64:128, 0:1], rs[ch][64:128, 0:n], ALU.mult, ALU.mult,
                           r=[ps_r, gn_r, rs_r[ch], qnz_r], w=[qna_r[bi]])
                elif ch == 1:
                    cx.stt(kna[:, t0:t0 + n], ps[:, 0:n], gn[:, 1:2], rs[ch][:, 0:n], ALU.mult, ALU.mult,
                           r=[ps_r, gn_r, rs_r[ch]], w=[kna_r[bi]])
                else:
                    cx.stt(xn[:, 0:n], ps[:, 0:n], gn[:, 2:3], rs[ch][:, 0:n], ALU.mult, ALU.mult,
                           r=[ps_r, gn_r, rs_r[ch]], w=[xn_r])

            def tail(bi=bi, t0=t0, n=n, i=i):
                p3, p3_r = cx.ps[5], cx.psr[5]
                cx.mm(p3[:, 0:n], rp[:], xn[:, 0:n], True, True, r=[rp_r, xn_r], w=[p3_r])
                cx.tt(t1[:, 0:n], xn[:, 0:n], cs[i][:, 0, 0:n], ALU.mult, r=[xn_r, cs_r[i]], w=[t1_r], eng="pool")
                cx.tt(t2[:, 0:n], p3[:, 0:n], cs[i][:, 1, 0:n], ALU.mult, r=[p3_r, cs_r[i]], w=[t2_r])
                cx.tt(qd[0:32, t0:t0 + n], t1[0:32, 0:n], t2[0:32, 0:n], ALU.add, r=[t1_r, t2_r, qz_r], w=[qd_r[bi]], eng="pool")
                cx.tt(qd2[32:64, t0:t0 + n], t1[32:64, 0:n], t2[32:64, 0:n], ALU.add, r=[t1_r, t2_r, qz_r], w=[qd_r[bi]], eng="pool")
                cx.tt(kd[0:64, t0:t0 + n], t1[64:128, 0:n], t2[64:128, 0:n], ALU.add, r=[t1_r, t2_r], w=[kd_r[bi]], eng="pool")
            pending[0] = tail
        pending[0]()

    if debug:
        dq_ = cx.dout("dbg_qna", [128, SE], BF16)
        dk_ = cx.dout("dbg_kna", [128, SE], BF16)
        dqd_ = cx.dout("dbg_qd", [64, SE], BF16)
        dkd_ = cx.dout("dbg_kd", [64, SE], BF16)
        dfu_ = cx.dout("dbg_fu", [64, SE], BF16)
        dvn_ = cx.dout("dbg_vna", [128, NKT, 2, 64], BF16)
        dvd_ = cx.dout("dbg_vd", [128, NKT, 128], BF16)
        cx.store(dq_[0:64], qna[0:64, :], r=qna_r, w=cx.res())
        cx.store(dq_[64:128], qna1[64:128, :], r=qna_r, w=cx.res())
        cx.store(dk_, kna[:], r=kna_r, w=cx.res())
        cx.store(dqd_, qd[0:64, :], r=qd_r, w=cx.res())
        cx.store(dkd_, kd[0:64, :], r=kd_r, w=cx.res())
        cx.store(dfu_, fuT[:], r=fu_r, w=cx.res())
        cx.store(dvn_, vna[:], r=v_r, w=cx.res())
        cx.store(dvd_, vd[:], r=v_r + [vd1_r], w=cx.res())
    if stop_after < 2:
        return cx.finish() if own else None
    def kblk(kt):
        return min(kt // 4, 16)

    cx.S.barrier()
    with ExitStack() as st3:
        def sb3(name, shape, dt):
            return st3.enter_context(nc.sbuf_tensor(cx.uname(name), list(shape), dt))
        fa = sb3("fa", [64, 128], BF16)
        fac = sb3("fac", [64, 128], BF16)
        fa_r, fac_r = cx.res(), cx.res()
        cx.load(fa[:], fa_d, w=fa_r)
        cx.load(fac[:], fac_d, w=fac_r)
        c128 = sb3("c128", [128, 256], F32)
        tab = sb3("tab", [128, 2, 128], F32)
        c64s = sb3("c64s", [128, 2, 64], F32)
        c256 = sb3("c256", [128, 2, 2, 256], F32)
        c128_r, tab_r, c64s_r, c256_r = cx.res(), cx.res(), cx.res(), cx.res()
        cx.load(c128[:], cs128_d, w=c128_r)
        cx.load(tab[:], tab_d, w=tab_r)
        cx.load(c64s[:], c64s_d, w=c64s_r)
        cx.load(c256[:], c256_d, w=c256_r)
        KQ = 16
        Aq = sb3("Aq", [128, 64, 2, KQ], F32)
        Aq_r = cx.res()
        Ure = sb3("Ure", [128, KQ, 128], F32)
        Uim = sb3("Uim", [128, KQ, 128], F32)
        Ure_r, Uim_r = cx.res(), cx.res()
        tm = [sb3("tm%d" % i, [128, 2, 128], F32) for i in range(4)]
        tm_r = [cx.res() for _ in range(4)]
        fo = sb3("fo", [KQ, S], BF16)
        fo_r = cx.res()
        fu_all = fu_r[0:16]
        fov = fo[:, :].rearrange("k (m2 m1) -> k m1 m2", m1=128)
        for q in range(64 // KQ):
            for g8 in range(8):
                pb = g8 % 2
                for n2i in range(8):
                    n2 = g8 * 8 + n2i
                    lhsT = fuT[:, n2:S:64]
                    for ri in range(2):
                        cx.mm(cx.ps[pb][:, (n2i * 2 + ri) * KQ:(n2i * 2 + ri + 1) * KQ], lhsT,
                              fa[:, ri * 64 + q * KQ:ri * 64 + (q + 1) * KQ], True, True, r=fu_all + [fa_r], w=[cx.psr[pb]])
                cx.copy(Aq[:, g8 * 8:(g8 + 1) * 8, :, :], cx.ps[pb][:, 0:8 * 2 * KQ].rearrange("p (a b c) -> p a b c", a=8, b=2),
                        r=[cx.psr[pb]], w=[Aq_r], eng=("act" if g8 % 2 else "dve"))
            for k2 in range(KQ // 2):
                pb = 2 + k2 % 2
                for kk in range(2):
                    kc = k2 * 2 + kk
                    lhsT = Aq[:, :, :, kc].rearrange("p a b -> p (a b)")
                    cx.mm(cx.ps[pb][:, kk * 256:(kk + 1) * 256], lhsT, c128[:, :], True, True, r=[Aq_r, c128_r], w=[cx.psr[pb]])
                pv_ = cx.ps[pb][:, :].rearrange("p (k c m) -> p k c m", k=2, c=2)
                Pc, Ps = pv_[:, :, 0, :], pv_[:, :, 1, :]
                TA = bc_mid(tab[:, 0, :], 2)
                TBt = bc_mid(tab[:, 1, :], 2)
                ksl = slice(k2 * 2, k2 * 2 + 2)
                cx.tt(tm[0][:], Pc, TA, ALU.mult, r=[cx.psr[pb], tab_r], w=[tm_r[0]])
                cx.tt(tm[1][:], Ps, TBt, ALU.mult, r=[cx.psr[pb], tab_r], w=[tm_r[1]])
                cx.tt(Ure[:, ksl, :], tm[0][:], tm[1][:], ALU.add, r=[tm_r[0], tm_r[1]], w=[Ure_r], eng="pool")
                cx.tt(tm[2][:], Pc, TBt, ALU.mult, r=[cx.psr[pb], tab_r], w=[tm_r[2]])
                cx.tt(tm[3][:], Ps, TA, ALU.mult, r=[cx.psr[pb], tab_r], w=[tm_r[3]])
                cx.tt(Uim[:, ksl, :], tm[2][:], tm[3][:], ALU.subtract, r=[tm_r[2], tm_r[3]], w=[Uim_r], eng="pool")
            for g in range(16):
                pb = 4 + g % 2
                for mi in range(8):
                    m1 = g * 8 + mi
                    cx.mm(cx.ps[pb][0:KQ, mi * 64:(mi + 1) * 64], Ure[:, :, m1], c64s[:, 0, :], True, False,
                          r=[Ure_r, c64s_r], w=[cx.psr[pb]])
                    cx.mm(cx.ps[pb][0:KQ, mi * 64:(mi + 1) * 64], Uim[:, :, m1], c64s[:, 1, :], False, True,
                          r=[Uim_r, c64s_r], w=[cx.psr[pb]])
                cx.copy(fov[:, g * 8:(g + 1) * 8, :], cx.ps[pb][0:KQ, 0:512].rearrange("p (a b) -> p a b", a=8),
                        r=[cx.psr[pb]], w=[fo_r], eng=("act" if g % 2 else "dve"))
            y_store(192 + q * KQ, 192 + (q + 1) * KQ, 0, S, fo[:, :], [fo_r], cx.stok())
        if ctxq:
            Ac = sb3("Ac", [128, 2, 128], F32)
            Ac_r = cx.res()
            for t_ in range(2):
                cx.mm(cx.ps[6][:, t_ * 128:(t_ + 1) * 128], fuT[:, S + t_ * 128:S + (t_ + 1) * 128], fac[:, :], True, True,
                      r=[fu_r[16], fac_r], w=[cx.psr[6]])
            cx.copy(Ac[:], cx.ps[6][:, 0:256].rearrange("p (t c) -> p t c", t=2), r=[cx.psr[6]], w=[Ac_r])
            k = 0
            for t_ in range(2):
                for ri in range(2):
                    cx.mm(cx.ps[7][0:64, 0:256], Ac[:, t_, ri * 64:(ri + 1) * 64], c256[:, t_, ri, :], k == 0, k == 3,
                          r=[Ac_r, c256_r], w=[cx.psr[7]])
                    k += 1
            fcs = sb3("fcs", [64, 256], BF16)
            fcs_r = cx.res()
            cx.copy(fcs[:], cx.ps[7][0:64, 0:256], r=[cx.psr[7]], w=[fcs_r])
            y_store(192, 256, S, NCTX, fcs[:], [fcs_r], cx.stok())
    st_fu.close()
    if io is not None and "after_fourier" in io:
        io["after_fourier"]()
    cx.S.barrier()
    yv_r = [cx.res() for _ in range(4)]
    with ExitStack() as st2:
        def sb2(name, shape, dt):
            return st2.enter_context(nc.sbuf_tensor(cx.uname(name), list(shape), dt))
        NP = 3
        pT = [sb2("pT%d" % i, [128, 1024], BF16) for i in range(NP)]
        pT_r = [cx.res() for _ in range(NP)]
        bres = sb2("bres", [128, 2, 8, 512], F32)
        bres_r = cx.res()
        cx.load(bres[:, 0], biasd[0, 1].rearrange("v p q -> p v q"), w=bres_r, q="sp")
        cx.load(bres[:, 1], biasd[1, 1].rearrange("v p q -> p v q"), w=bres_r, q="act")
        bt = [sb2("bt%d" % i, [128, 512], F32) for i in range(3)]
        bt_r = [cx.res() for _ in range(3)]
        sbf = [sb2("sbf%d" % i, [128, 512], F32) for i in range(3)]
        sbf_r = [cx.res() for _ in range(3)]
        yst = [sb2("yst%d" % i, [128, 512], BF16) for i in range(2)]
        yst_r = [cx.res() for _ in range(2)]
        rz = sb2("rz", [128, 512], F32)
        rz_r = cx.res()
        rz2 = sb2("rz2", [128, 512], F32)
        rz2_r = cx.res()
        a1 = sb2("a1", [64, 512], F32)
        a2 = sb2("a2", [64, 512], F32)
        a1_r, a2_r = cx.res(), cx.res()
        ydst = [sb2("ydst%d" % i, [64, 512], BF16) for i in range(2)]
        ydst_r = [cx.res() for _ in range(2)]
        b64n = sb2("b64n", [64, 64], F32)
        b64n_r = cx.res()
        cx.copy(b64n[:], b64[0:64, 0:64], r=[b64_r], w=[b64n_r])
        SC_NA = 64 ** -0.5
        SC_D = 32 ** -0.5
        qblocks = list(range(16)) + ([16] if ctxq else [])
        st_ = dict(pc=0, bc=0, sc=0)
        for qi, qb in enumerate(qblocks):
            t0, n = TB[qb]
            ys, ys_r = yst[qi % 2], yst_r[qi % 2]
            for h in range(2):
                tiles = (na_tiles(qb) if qb < 16 else []) + [(64, None, None), (65, None, None)]
                po, po_r = cx.ps[4 + (h % 2) * 2], cx.psr[4 + (h % 2) * 2]
                pz, pz_r = cx.ps[5 + (h % 2) * 2], cx.psr[5 + (h % 2) * 2]
                hs = slice(64 * h, 64 * h + 64)

                def na_score(tile, h=h, hs=hs, t0=t0, n=n, qb=qb):
                    kt, vg, vi = tile
                    pb = st_["pc"] % 4
                    pss, pss_r = cx.ps[pb], cx.psr[pb]
                    pi = st_["pc"] % NP
                    st_["pc"] += 1
                    pt, pt_r = pT[pi], pT_r[pi]
                    cx.mm(pss[:, 0:n], kna[:, kt * 128:(kt + 1) * 128], (qna if h == 0 else qna1)[:, t0:t0 + n], True, True,
                          r=[kna_r[kblk(kt)], qna_r[qb], qnz_r], w=[pss_r])
                    if vg is not None:
                        s_, s_r = sbf[st_["sc"] % 3], sbf_r[st_["sc"] % 3]
                        st_["sc"] += 1
                        if vg == 1:
                            b_ap, b_r = bres[:, h, vi, 0:n], bres_r
                        else:
                            bb, b_r = bt[st_["bc"] % 3], bt_r[st_["bc"] % 3]
                            st_["bc"] += 1
                            cx.load(bb[:], biasd[h, vg, vi], w=b_r, q="sp")
                            b_ap = bb[:, 0:n]
                        cx.stt(s_[:, 0:n], pss[:, 0:n], SC_NA, b_ap, ALU.mult, ALU.add, r=[pss_r, b_r], w=[s_r])
                        cx.act(pt[:, 0:n], s_[:, 0:n], AF.Exp, r=[s_r], w=[pt_r])
                    else:
                        cx.act(pt[:, 0:n], pss[:, 0:n], AF.Exp, r=[pss_r], w=[pt_r], scale=SC_NA)
                    return pi

                LA_N = 2
                nq = [na_score(tiles[u]) for u in range(min(LA_N, len(tiles)))]
                for ti, (kt, vg, vi) in enumerate(tiles):
                    if ti + LA_N < len(tiles):
                        nq.append(na_score(tiles[ti + LA_N]))
                    cur = nq.pop(0)
                    pt, pt_r = pT[cur], pT_r[cur]
                    last = ti == len(tiles) - 1
                    cx.mm(po[:, 0:n], vna[:, kt, :, :].rearrange("p h e -> p (h e)"), pt[:, 0:n], ti == 0, last,
                          r=[v_r[kblk(kt)], pt_r], w=[po_r])
                    cx.mm(pz[:, 0:n], onb[:, :], pt[:, 0:n], ti == 0, last, r=[onb_r, pt_r], w=[pz_r])
                cx.recip(rz[hs, 0:n], pz[hs, 0:n], r=[pz_r], w=[rz_r])
                cx.tt(ys[hs, 0:n], po[hs, 0:n], rz[hs, 0:n], ALU.mult, r=[po_r, rz_r], w=[ys_r])
            y_store(0, 128, t0, n, ys[:, 0:n], [ys_r], yv_r[qi % 2])
        if io is not None and "after_na" in io:
            io["after_na"]()
        for qi, qb in enumerate(qblocks):
            t0, n = TB[qb]
            kts = list(range(NKT)) if qb < 16 else [64, 65]
            po1, po1_r = cx.ps[6], cx.psr[6]
            po2, po2_r = cx.ps[7], cx.psr[7]

            def d_score(kt, t0=t0, n=n, qb=qb):
                pp = st_["pc"] % 3
                pi = st_["pc"] % NP
                st_["pc"] += 1
                ks = slice(kt * 128, (kt + 1) * 128)
                for comp in range(2):
                    qsrc = qd if comp == 0 else qd2
                    cx.mm(cx.ps[2 * pp + comp][:, 0:n], kd[:, ks], qsrc[:, t0:t0 + n], True, True,
                          r=[kd_r[kblk(kt)], qd_r[qb], qz_r], w=[cx.psr[2 * pp + comp]])
                src = cx.ps_all[:, 1024 * pp:1024 * (pp + 1)].rearrange("p (c n) -> p c n", c=2)[:, :, 0:n]
                dst = pT[pi][:, :].rearrange("p (c n) -> p c n", c=2)[:, :, 0:n]
                cx.act(dst, src, AF.Exp, r=[cx.psr[2 * pp], cx.psr[2 * pp + 1]], w=[pT_r[pi]], scale=SC_D)
                return pi

            LA = 2
            queue = [d_score(kts[u]) for u in range(min(LA, len(kts)))]
            for ti, kt in enumerate(kts):
                if ti + LA < len(kts):
                    queue.append(d_score(kts[ti + LA]))
                cur = queue.pop(0)
                last = ti == len(kts) - 1
                pt, pt_r = pT[cur], pT_r[cur]
                cx.mm(po1[:, 0:n], vd[:, kt, :], pt[:, 0:n], ti == 0, last, r=[v_r[kblk(kt)], vd1_r, pt_r], w=[po1_r])
                cx.mm(po2[:, 0:n], vd[:, kt, :], pt[:, 512:512 + n], ti == 0, last, r=[v_r[kblk(kt)], vd1_r, pt_r], w=[po2_r])
            cx.recip(rz[64:128, 0:n], po1[64:128, 0:n], r=[po1_r], w=[rz_r])
            cx.recip(rz2[64:128, 0:n], po2[64:128, 0:n], r=[po2_r], w=[rz2_r])
            cx.tt(a1[:, 0:n], po1[0:64, 0:n], rz[64:128, 0:n], ALU.mult, r=[po1_r, rz_r], w=[a1_r])
            cx.tt(a2[:, 0:n], po2[0:64, 0:n], rz2[64:128, 0:n], ALU.mult, r=[po2_r, rz2_r], w=[a2_r])
            cx.stt(a1[:, 0:n], a2[:, 0:n], nlam[:, 0:1], a1[:, 0:n], ALU.mult, ALU.add, r=[a2_r, nl_r, a1_r], w=[a1_r])
            cx.act(a2[:, 0:n], a1[:, 0:n], AF.Square, r=[a1_r], w=[a2_r])
            p6, p6_r = cx.ps[0], cx.psr[0]
            cx.mm(p6[0:64, 0:n], b64n[:, :], a2[:, 0:n], True, True, r=[b64n_r, a2_r], w=[p6_r])
            cx.act(rz[0:64, 0:n], p6[0:64, 0:n], AF.Sqrt, r=[p6_r], w=[rz_r], bias=EPS)
            cx.recip(rz[0:64, 0:n], rz[0:64, 0:n], r=[rz_r], w=[rz_r])
            yd, yd_r = ydst[qi % 2], ydst_r[qi % 2]
            cx.stt(yd[:, 0:n], a1[:, 0:n], gsub[:, 0:1], rz[0:64, 0:n], ALU.mult, ALU.mult, r=[a1_r, gs_r, rz_r], w=[yd_r])
            y_store(128, 192, t0, n, yd[:, 0:n], [yd_r], yv_r[2 + qi % 2])

    if io is not None and "after_diff" in io:
        io["after_diff"]()
    return cx.finish() if own else None


_CONST = {}


def mixer_consts():
    if _CONST:
        return _CONST
    pos = np.arange(S)
    rows, cols = pos // 64, pos % 64
    inv = (np.float32(10000.0) ** (-np.arange(0, 16, 2, dtype=np.float32) / np.float32(16))).astype(np.float32)
    cos = np.ones((128, SE), np.float32)
    sin = np.zeros((128, SE), np.float32)
    for p in range(128):
        d = p % 32
        part, dd = d // 16, d % 16
        i, half = dd % 8, dd // 8
        ang = ((rows if part == 0 else cols).astype(np.float32) * inv[i]).astype(np.float32)
        cos[p, :S] = np.cos(ang)
        sin[p, :S] = (-np.sin(ang)) if half == 0 else np.sin(ang)
    rperm = np.zeros((128, 128), np.float32)
    for p in range(128):
        q = p + 8 if (p % 16) < 8 else p - 8
        rperm[q, p] = 1.0
    _CONST["cos"], _CONST["sin"], _CONST["rperm"] = cos, sin, rperm
    _CONST["bo64"] = (np.kron(np.eye(2), np.ones((64, 64))) / 64.0).astype(np.float32)
    _CONST["bo32"] = (np.kron(np.eye(4), np.ones((32, 32))) / 32.0).astype(np.float32)
    _CONST["ones_bf"] = np.ones((128, 128), NPBF)
    two_pi = 2.0 * np.pi
    ch = np.arange(64)
    a64 = two_pi * np.outer(ch, ch) / 64.0
    base = np.concatenate([np.cos(a64), -np.sin(a64)], axis=1)
    _CONST["f_rhs_a"] = (base / math.sqrt(S * 64.0)).astype(NPBF)
    _CONST["f_rhs_ac"] = (base / math.sqrt(NCTX * 64.0)).astype(NPBF)
    n1 = np.arange(128)
    a128 = two_pi * np.outer(n1, n1) / 128.0
    _CONST["f_cs128"] = np.concatenate([np.cos(a128), np.sin(a128)], axis=1).astype(np.float32)
    jj = np.arange(128)
    n2, ri = jj // 2, jj % 2
    at = two_pi * np.outer(n2, n1) / float(S)
    Tr, Ti = np.cos(at), np.sin(at)
    TA = np.where(ri[:, None] == 0, Tr, Ti)
    TBm = np.where(ri[:, None] == 0, -Ti, Tr)
    _CONST["f_tab"] = np.ascontiguousarray(np.stack([TA, TBm], axis=1)).astype(np.float32)
    a64s = two_pi * np.outer(n2, ch) / 64.0
    _CONST["f_cs64s"] = np.ascontiguousarray(np.stack([np.cos(a64s), np.sin(a64s)], axis=1)).astype(np.float32)
    n = np.arange(256)
    a256 = two_pi * np.outer(n, n) / 256.0
    cs = np.stack([np.cos(a256), np.sin(a256)], axis=1)
    _CONST["f_cs256"] = np.ascontiguousarray(cs.reshape(2, 128, 2, 256).transpose(1, 0, 2, 3)).astype(np.float32)
    idx = {}
    for g, qb, kts in ((0, 0, range(0, 6)), (1, 1, range(2, 10)), (2, 15, range(58, 64))):
        q = np.arange(512)
        qr, qc = 8 * qb + q // 64, q % 64
        rs = np.clip(qr - 4, 0, 120)
        csx = np.clip(qc - 8, 0, 48)
        for vi, kt in enumerate(kts):
            k = np.arange(128)
            kr, kc = 2 * kt + k // 64, k % 64
            valid = ((kr[:, None] >= rs[None, :]) & (kr[:, None] < rs[None, :] + 8) &
                     (kc[:, None] >= csx[None, :]) & (kc[:, None] < csx[None, :] + 16))
            dr = np.clip(kr[:, None] - qr[None, :] + 7, 0, 14)
            dc = np.clip(kc[:, None] - qc[None, :] + 15, 0, 30)
            idx[(g, vi)] = (valid, dr, dc)
    _CONST["na_idx"] = idx
    return _CONST


def mixer_inputs(inp, l, j, hT_full):
    C = mixer_consts()
    wi = inp["w_in"][l]
    h0, h1 = 2 * j, 2 * j + 1
    cols = []
    for base in (0, 512):
        cols += list(range(base + h0 * 64, base + h0 * 64 + 64)) + list(range(base + h1 * 64, base + h1 * 64 + 64))
    cols += list(range(1536 + j * 64, 1536 + j * 64 + 64)) + list(range(1792 + j * 64, 1792 + j * 64 + 64))
    cols += list(range(2304 + j * 64, 2304 + j * 64 + 64))
    cols += list(range(1024 + h0 * 64, 1024 + h0 * 64 + 64)) + list(range(1024 + h1 * 64, 1024 + h1 * 64 + 64))
    cols += list(range(2048 + j * 64, 2048 + j * 64 + 64))
    gains = np.zeros((128, 4), np.float32)
    gains[:, 0] = np.tile(inp["na_q_g"][l], 2)
    gains[:, 1] = np.tile(inp["na_k_g"][l], 2)
    gains[:, 2] = np.concatenate([inp["diff_q_g"][l]] * 2 + [inp["diff_k_g"][l]] * 2)
    gains[0:64, 3] = inp["diff_subln_g"][l]
    rpb = inp["na_rpb"][l]
    bias = np.zeros((2, 3, 8, 128, 512), np.float32)
    for (g, vi), (valid, dr, dc) in C["na_idx"].items():
        for hh, h in enumerate((h0, h1)):
            bias[hh, g, vi] = np.where(valid, rpb[h][dr, dc], np.float32(NEG))
    d = {
        "hT": hT_full, "w_in": np.ascontiguousarray(wi[:, cols]), "gains": gains,
        "lam": np.ascontiguousarray(np.broadcast_to(inp["diff_lambda"][l][None], (64, 4, 32))).astype(np.float32),
        "bias": bias,
    }
    for k in ("cos", "sin", "rperm", "bo64", "bo32", "ones_bf", "f_rhs_a", "f_rhs_ac", "f_cs128", "f_tab", "f_cs64s", "f_cs256"):
        d[k] = C[k]
    return d


RG = [[0, 1, 2, 3], [4, 5, 6, 7]]
HS = 2048 + NCTX
YW = 2 + S + 2 + NCTX
YCTX0 = S + 4


def build_fused(stop=99):
    cx = Ctx()
    nc = cx.nc
    modsb = cx.sb("modsb", [128, DEPTH, 48, 2], F32)
    mod_r = cx.res("modsb")
    xo = cx.dout("xTo", [D, W], F32)
    xscr = cx.dram("xscr", [D, W], F32)
    xscr_r = cx.res("xscr")
    NYC, YR = 8, 32
    HCW = [512, 512, 512, 512, NCTX]
    hsend = [[cx.dram("hsend%d_%d" % (l, k), [D, HCW[k]], BF16) for k in range(5)] for l in range(DEPTH)]
    hgath = [[cx.dram("hgath%d_%d" % (l, k), [4 * D, HCW[k]], BF16) for k in range(5)] for l in range(DEPTH)]
    ysend = [[cx.dram("ysend%d_%d" % (l, k), [YR, YW], BF16) for k in range(NYC)] for l in range(DEPTH)]
    ygath = [[cx.dram("ygath%d_%d" % (l, k), [4 * YR, YW], BF16) for k in range(NYC)] for l in range(DEPTH)]
    yloc = [[cx.dram("yloc%d_%d" % (l, k), [4 * YR, LATW], BF16) for k in range(NYC)] for l in range(DEPTH)]
    hsend_r = [[cx.res() for k in range(5)] for l in range(DEPTH)]
    hgath_r = [[cx.res() for k in range(5)] for l in range(DEPTH)]
    ysend_r = [[cx.res() for k in range(NYC)] for l in range(DEPTH)]
    ygath_r = [[cx.res() for k in range(NYC)] for l in range(DEPTH)]
    yloc_r = [[cx.res() for k in range(NYC)] for l in range(DEPTH)]
    dyn = {}

    def pre_emit():
        dyn["r"] = nc.sync.snap((nc.sync.partition_id() % 4) * 2048, min_val=0, max_val=3 * 2048)
    cx.S.pre_emit = pre_emit

    def gather(src, src_r, dst, dst_r):
        cx.S.dma("pool", lambda: nc.gpsimd.collective_compute("AllGather", ALU.bypass, replica_groups=RG,
                                                            ins=[src.ap().opt()], outs=[dst.ap().opt()]),
                 r=[src_r], w=dst_r, inc=1)

    def mk_h_store(l):
        sent = [False] * 5

        def h_store(src, src_r, c0, c1, base):
            a, b_ = max(c0, HALO), min(c1, HALO + 2048)
            while a < b_:
                k = (a - HALO) // 512
                e = min(b_, HALO + (k + 1) * 512)
                hv = hsend[l][k].ap().rearrange("(kc p) n -> p kc n", p=128)
                cx.store(hv[:, :, a - HALO - 512 * k:e - HALO - 512 * k], src[:, :, a - base:e - base], r=[src_r], w=hsend_r[l][k])
                a = e
            a, b_ = max(c0, CTX0), min(c1, CTX0 + NCTX)
            if a < b_:
                hv = hsend[l][4].ap().rearrange("(kc p) n -> p kc n", p=128)
                cx.store(hv[:, :, a - CTX0:b_ - CTX0], src[:, :, a - base:b_ - base], r=[src_r], w=hsend_r[l][4], q="act")
            for k in range(5):
                end_col = HALO + (k + 1) * 512 if k < 4 else CTX0 + NCTX
                if not sent[k] and c1 >= end_col:
                    sent[k] = True
                    gather(hsend[l][k], hsend_r[l][k], hgath[l][k], hgath_r[l][k])
        return h_store

    def mk_h_load(l):
        def h_load(t_, t_r, bi):
            t0_, n_ = TB[bi]
            rnk, k = (bi // 4, bi % 4) if bi < 16 else (0, 4)
            cx.load(t_[:, :, 0:n_], hgath[l][k].ap()[rnk * D:(rnk + 1) * D, :].rearrange("(kc p) n -> p kc n", p=128), w=t_r,
                    r=[hgath_r[l][k]])
        return h_load

    def mk_y_store(l):
        def y_store(r0, r1, t0_, n_, src, src_r, w):
            c = 2 + t0_ if t0_ < S else YCTX0 + (t0_ - S)
            rr = r0
            while rr < r1:
                k = rr // YR
                e = min(r1, (k + 1) * YR)
                cx.store(ysend[l][k].ap()[rr - k * YR:e - k * YR, c:c + n_], src[rr - r0:e - r0], r=src_r, w=ysend_r[l][k],
                         q=("sp" if k % 2 == 0 else "act"))
                rr = e
        return y_store

    def y_exchange(l, ks):
        for k in ks:
            gather(ysend[l][k], ysend_r[l][k], ygath[l][k], ygath_r[l][k])
        for k in ks:
            cx.S.dma("sp", (lambda l=l, k=k: nc.sync.dma_start(
                out=yloc[l][k].ap(), in_=ygath[l][k].ap()[:, bass.ds(dyn["r"], LATW)])), r=[ygath_r[l][k]], w=yloc_r[l][k])

    def mk_y_load(l, last):
        def y_load(t_, parts, c0, c1):
            n = c1 - c0
            cx.memset(t_[:, :, 0:n], 0.0, w=parts, eng="pool")
            pieces = []
            a, b_ = max(c0, 0), min(c1, LATW)
            if a < b_:
                pieces.append((a, b_, True))
            a, b_ = max(c0, CTX0), min(c1, CTX0 + NCTX)
            if a < b_ and not last:
                pieces.append((a, b_, False))
            qi = 0
            for (a, b_, lat) in pieces:
                dsl = slice(a - c0, b_ - c0)
                for k in range(NYC):
                    if lat:
                        srcv, csl, sr = yloc[l][k].ap(), slice(a, b_), yloc_r[l][k]
                    else:
                        srcv, csl, sr = ygath[l][k].ap(), slice(YCTX0 + a - CTX0, YCTX0 + b_ - CTX0), ygath_r[l][k]
                    v = srcv.rearrange("(r q) n -> q r n", q=YR)
                    if k < 4:
                        specs = [(t_[32 * k:32 * k + 32, 0:4, dsl], slice(0, 4))]
                    else:
                        kc0 = 4 if k < 6 else 6
                        p0 = 32 * (k % 2)
                        specs = [(t_[p0:p0 + 32, kc0:kc0 + 2, dsl], slice(0, 4, 2)),
                                 (t_[64 + p0:64 + p0 + 32, kc0:kc0 + 2, dsl], slice(1, 4, 2))]
                    for (dst, rsl) in specs:
                        cx.load(dst, v[:, rsl, csl], w=parts[qi], r=[sr], q=("sp" if qi % 2 == 0 else "act"))
                        qi += 1
        return y_load

    cx.begin_phase()
    build_prep(cx=cx, io=dict(modsb=modsb, mod_r=mod_r, h_store=mk_h_store(0)))
    cx.end_phase()
    zt = cx.sb("zt", [128, 4], BF16)
    zt_r = cx.res("zt")
    cx.memset(zt[:], 0.0, w=[zt_r], eng="pool")
    for l in range(DEPTH):
        last = l == DEPTH - 1
        sfx = "_l%d" % l
        lam_init = 0.8 - 0.6 * math.exp(-0.3 * l)
        cx.begin_phase()
        for k in range(NYC):
            for c in (0, S + 2):
                cx.store(ysend[l][k].ap()[:, c:c + 2], zt[0:YR, 0:2], r=[zt_r], w=ysend_r[l][k])
        mio = dict(h_load=mk_h_load(l), y_store=mk_y_store(l),
                   after_fourier=(lambda l=l: y_exchange(l, range(6, 8))), after_na=(lambda l=l: y_exchange(l, range(0, 4))))
        build_mixer(not last, lam_init, cx=cx, io=mio, sfx=sfx)
        cx.end_phase()
        y_exchange(l, range(4, 6))
        cx.begin_phase()
        xin = cx.din("xT", [D, W], F32) if l == 0 else xscr.ap()
        io = dict(x_src=xin.rearrange("(kc p) n -> p kc n", p=128), x_src_r=([] if l == 0 else [xscr_r]),
                  x_dst=(xo if last else xscr.ap()).rearrange("(kc p) n -> p kc n", p=128),
                  mod_t=modsb[:, l], mod_r=mod_r, y_load=mk_y_load(l, last))
        if not last:
            io["x_dst_r"] = xscr_r
            io["modn_t"] = modsb[:, l + 1]
            io["h_store"] = mk_h_store(l + 1)
        build_post(last, cx=cx, io=io, sfx=sfx, sfxn="_l%d" % (l + 1))
        cx.end_phase()
    nc_ = cx.finish()
    print("fused program: ops=%d sems=%d" % (len(cx.S.ops), cx.S.nsem))
    return nc_


def fused_inputs(inp, b, j):
    d = prep_inputs(inp, b, j)
    d["mask"] = col_mask(j)
    for l in range(DEPTH):
        sfx = "_l%d" % l
        m = mixer_inputs(inp, l, j, None)
        for k in ("w_in", "gains", "lam", "bias"):
            d[k + sfx] = m[k]
        for k in ("cos", "sin", "rperm", "bo64", "bo32", "ones_bf", "f_rhs_a", "f_rhs_ac", "f_cs128", "f_tab", "f_cs64s", "f_cs256"):
            d[k] = m[k]
        p = post_inputs(inp, l, None, None, None, None, None, True)
        for k in ("g2", "w_out", "w_four", "w_up", "cw", "cb", "w_down"):
            d[k + sfx] = p[k]
        if l + 1 < DEPTH:
            d["g1n_l%d" % (l + 1)] = pvec(inp["norm1_g"][l + 1], 8)
    return d


def kernel(**inputs):
    inp = {k: np.asarray(v) for k, v in inputs.items()}
    cores = [(b, j) for b in range(2) for j in range(4)]
    res = run_prog(build_fused(), [fused_inputs(inp, b, j) for (b, j) in cores])
    out = np.zeros((2, S, D), np.float32)
    for ci, (b, j) in enumerate(cores):
        out[b, 2048 * j:2048 * j + 2048] = np.asarray(res[ci]["xTo"])[:, HALO:HALO + 2048].T
    return out


def kernel_unfused(**inputs):
    inp = {k: np.asarray(v) for k, v in inputs.items()}
    cores = [(b, j) for b in range(2) for j in range(4)]
    r0 = run_prog(build_prep(), [prep_inputs(inp, b, j) for (b, j) in cores])
    modT = [np.asarray(r["modT"]) for r in r0]
    hloc = [np.asarray(r["hT"]) for r in r0]
    xloc = [local_xT(inp["x"][b], inp["ctx"][b], j) for (b, j) in cores]
    masks = [col_mask(j) for j in range(4)]
    for l in range(DEPTH):
        last = l == DEPTH - 1
        lam_init = 0.8 - 0.6 * math.exp(-0.3 * l)
        hfull = []
        for b in range(2):
            parts = [hloc[4 * b + j][:, HALO:HALO + 2048] for j in range(4)] + [hloc[4 * b][:, CTX0:CTX0 + NCTX]]
            hfull.append(np.ascontiguousarray(np.concatenate(parts, axis=1)))
        rm = run_prog(build_mixer(not last, lam_init), [mixer_inputs(inp, l, j, hfull[b]) for (b, j) in cores])
        yfull = []
        for b in range(2):
            y = np.zeros((D, SE), NPBF)
            ncol = SE if not last else S
            for j in range(4):
                yc = np.asarray(rm[4 * b + j]["yT"])
                y[128 * j:128 * j + 128, :ncol] = yc[0:128, :ncol]
                y[512 + 64 * j:512 + 64 * j + 64, :ncol] = yc[128:192, :ncol]
                y[768 + 64 * j:768 + 64 * j + 64, :ncol] = yc[192:256, :ncol]
            yfull.append(y)
        ims = []
        for ci, (b, j) in enumerate(cores):
            ims.append(post_inputs(inp, l, xloc[ci], local_cols(yfull[b], j), np.ascontiguousarray(modT[ci][l]),
                                   None if last else np.ascontiguousarray(modT[ci][l + 1]), masks[j], last))
        rp = run_prog(build_post(last), ims)
        xloc = [np.asarray(r["xTo"]) for r in rp]
        if not last:
            hloc = [np.asarray(r["hTo"]) for r in rp]
    out = np.zeros((2, S, D), np.float32)
    for ci, (b, j) in enumerate(cores):
        out[b, 2048 * j:2048 * j + 2048] = xloc[ci][:, HALO:HALO + 2048].T
    return out
```

```python
import math
from contextlib import ExitStack

import numpy as np
import ml_dtypes

import concourse.bass as bass
import concourse.mybir as mybir
from concourse.bass_utils import run_bass_kernel_spmd

LAST_DINS = []
F32 = mybir.dt.float32
BF16 = mybir.dt.bfloat16
AF = mybir.ActivationFunctionType
ALU = mybir.AluOpType
AX = mybir.AxisListType
NPBF = ml_dtypes.bfloat16

D = 1024
S = 8192
NCTX = 256
SE = S + NCTX
DEPTH = 2
DFF = 2816
EPS = 1e-6
NCORES = 8
HALO = 2
LATW = 2048 + 2 * HALO
PAD0 = LATW
CTX0 = LATW + 1
W = CTX0 + NCTX + 1
DISJ = [(0, 512), (512, 1024), (1024, 1536), (1536, 2048), (2048, W)]
FFNB = [(1, 511), (511, 1021), (1021, 1531), (1531, 2041), (2041, W - 1)]
NEG = -30000.0


def segs(c0, c1):
    out = []
    if c0 < CTX0:
        out.append((c0, min(c1, CTX0), 0))
    if c1 > CTX0:
        out.append((max(c0, CTX0), c1, 1))
    return out


class Res:
    __slots__ = ("name", "last_w", "rd", "rd_dma", "sem", "ndma", "base", "last_use", "retired")

    def __init__(self, name):
        self.name = name
        self.last_w = None
        self.rd = {}
        self.rd_dma = []
        self.sem = None
        self.ndma = 0
        self.base = 0
        self.last_use = -1
        self.retired = False


class _Op:
    __slots__ = ("eng", "fn", "deps", "dres", "ms", "k", "inc")


class Sched:
    def __init__(self, nc, stack):
        self.nc = nc
        self.stack = stack
        self.ops = []
        self.eo = dict(pe=nc.tensor, act=nc.scalar, dve=nc.vector, pool=nc.gpsimd, sp=nc.sync)
        self.stores = []
        self.frontier = set()
        self.bars = []
        self.last_eng = {}
        self.dma_last = {}

    def barrier(self):
        self.frontier = set(self.last_eng.values()) | set(self.dma_last.values())
        self.bars.append(len(self.ops))

    def _deps(self, eng, r, w, acc):
        deps = set(self.frontier)
        for x in r:
            if x.last_w is not None:
                deps.add(x.last_w)
        for x in w:
            if x.last_w is not None:
                lw = self.ops[x.last_w]
                if not (acc and eng == "pe" and lw.eng == "pe" and lw.dres is None):
                    deps.add(x.last_w)
            deps.update(x.rd.values())
            deps.update(x.rd_dma)
        return deps

    def op(self, eng, fn, r=(), w=(), acc=False):
        o = _Op()
        o.eng, o.fn, o.dres, o.ms, o.k = eng, fn, None, False, 0
        o.deps = self._deps(eng, r, w, acc)
        idx = len(self.ops)
        self.ops.append(o)
        self.last_eng[eng] = idx
        for x in r:
            x.rd[eng] = idx
            x.last_use = idx
        for x in w:
            x.last_use = idx
        for x in w:
            x.last_w = idx
            x.rd = {}
            x.rd_dma = []
        return idx

    def dma(self, q, fn, r=(), w=None, store=False, inc=16):
        o = _Op()
        o.eng, o.fn, o.dres, o.ms = q, fn, w, False
        o.inc = inc
        o.deps = self._deps(q, r, (w,), False)
        idx = len(self.ops)
        self.ops.append(o)
        w.ndma += inc
        o.k = w.ndma
        self.dma_last[id(w)] = idx
        w.last_use = idx
        for x in r:
            x.rd_dma.append(idx)
            x.last_use = idx
        w.last_w = idx
        w.rd = {}
        w.rd_dma = []
        if store:
            self.stores.append(idx)
        return idx

    def emit(self):
        nc, ops = self.nc, self.ops
        if getattr(self, "pre_emit", None) is not None:
            self.pre_emit()
        fin = _Op()
        fin.eng, fin.fn, fin.dres, fin.ms, fin.k = "sp", None, None, False, 0
        fin.deps = set(self.stores)
        ops.append(fin)
        for o in ops:
            for d in o.deps:
                if ops[d].dres is None:
                    ops[d].ms = True
        esem = {}
        cnt = {}
        for e in self.eo:
            esem[e] = self.stack.enter_context(nc.semaphore("s_" + e))
            cnt[e] = 0
        waited = {e: {} for e in self.eo}
        nsem = len(esem)
        free_sems = []
        sem_res = []
        bars = set(self.bars)
        for oi, o in enumerate(ops):
            if oi in bars:
                for rs_ in sem_res:
                    if not rs_.retired and rs_.last_use < oi:
                        rs_.retired = True
                        free_sems.append((rs_.sem, rs_.base + rs_.ndma))
            ws = {}
            for d in o.deps:
                dop = ops[d]
                if dop.dres is None:
                    key, sem, val = dop.eng, esem[dop.eng], dop.k
                else:
                    rs = dop.dres
                    key, sem, val = id(rs), rs.sem, rs.base + dop.k
                if waited[o.eng].get(key, 0) >= val:
                    continue
                if key not in ws or ws[key][1] < val:
                    ws[key] = (sem, val)
            eobj = self.eo[o.eng]
            for key, (sem, val) in ws.items():
                eobj.wait_ge(sem, val)
                waited[o.eng][key] = val
            if o.fn is None:
                continue
            ins = o.fn()
            if o.dres is not None:
                rs = o.dres
                if rs.sem is None:
                    if free_sems and o.inc == 16:
                        rs.sem, rs.base = free_sems.pop()
                    else:
                        rs.sem = self.stack.enter_context(nc.semaphore("d%d" % nsem))
                        nsem += 1
                    if o.inc == 16:
                        sem_res.append(rs)
                ins.then_inc(rs.sem, o.inc)
            elif o.ms:
                cnt[o.eng] += 1
                o.k = cnt[o.eng]
                ins.then_inc(esem[o.eng], 1)
        self.nsem = nsem
        assert nsem < 240, nsem


class Ctx:
    def __init__(self):
        self.nc = bass.Bass("TRN2", target_bir_lowering=False)
        self.stack = ExitStack()
        self.S = Sched(self.nc, self.stack)
        nc = self.nc
        self.ps_all = nc.alloc_psum_tensor("ps_all", [128, 4096], F32)
        self.ps = [self.ps_all[:, 512 * i:512 * (i + 1)] for i in range(8)]
        self.psr = [Res("psb%d" % i) for i in range(8)]
        self.nres = 0
        self.scope = self.stack
        self.dins = {}
        self.stoks = [Res("stok%d" % i) for i in range(6)]
        self.nst = 0

    def stok(self):
        self.nst += 1
        return self.stoks[self.nst % len(self.stoks)]

    def begin_phase(self):
        self.S.barrier()
        self.scope = ExitStack()

    def end_phase(self):
        self.scope.close()
        self.scope = self.stack

    def uname(self, name):
        self.nres += 1
        return "%s_%d" % (name, self.nres)

    def dram(self, name, shape, dt):
        return self.nc.dram_tensor(name, list(shape), dt)

    def res(self, name=None):
        self.nres += 1
        return Res(name or ("r%d" % self.nres))

    def din(self, name, shape, dt):
        if name not in self.dins:
            self.dins[name] = self.nc.dram_tensor(name, list(shape), dt, kind="ExternalInput").ap()
        return self.dins[name]

    def dout(self, name, shape, dt):
        return self.nc.dram_tensor(name, list(shape), dt, kind="ExternalOutput").ap()

    def sb(self, name, shape, dt):
        self.nres += 1
        return self.scope.enter_context(self.nc.sbuf_tensor("%s_%d" % (name, self.nres), list(shape), dt))

    def mm(self, out, lhsT, rhs, start, stop, r, w):
        nc = self.nc
        return self.S.op("pe", lambda: nc.tensor.matmul(out, lhsT=lhsT, rhs=rhs, start=start, stop=stop), r=r, w=w, acc=True)

    def act(self, out, in_, func, r, w, scale=1.0, bias=0.0):
        nc = self.nc
        return self.S.op("act", lambda: nc.scalar.activation(out=out, in_=in_, func=func, bias=bias, scale=scale), r=r, w=w)

    def tt(self, out, in0, in1, op, r, w, eng="dve"):
        e = self.nc.vector if eng == "dve" else self.nc.gpsimd
        return self.S.op(eng, lambda: e.tensor_tensor(out=out, in0=in0, in1=in1, op=op), r=r, w=w)

    def stt(self, out, in0, scalar, in1, op0, op1, r, w, eng="dve"):
        e = self.nc.vector if eng == "dve" else self.nc.gpsimd
        return self.S.op(eng, lambda: e.scalar_tensor_tensor(out=out, in0=in0, scalar=scalar, in1=in1, op0=op0, op1=op1), r=r, w=w)

    def ts(self, out, in0, s1, s2, op0, op1, r, w, eng="dve"):
        e = self.nc.vector if eng == "dve" else self.nc.gpsimd
        if s2 is None:
            return self.S.op(eng, lambda: e.tensor_scalar(out=out, in0=in0, scalar1=s1, scalar2=None, op0=op0), r=r, w=w)
        return self.S.op(eng, lambda: e.tensor_scalar(out=out, in0=in0, scalar1=s1, scalar2=s2, op0=op0, op1=op1), r=r, w=w)

    def recip(self, out, in_, r, w):
        nc = self.nc
        return self.S.op("dve", lambda: nc.vector.reciprocal(out=out, in_=in_), r=r, w=w)

    def copy(self, out, in_, r, w, eng="dve"):
        if eng == "act":
            return self.act(out, in_, AF.Copy, r, w)
        e = self.nc.vector if eng == "dve" else self.nc.gpsimd
        return self.S.op(eng, lambda: e.tensor_copy(out=out, in_=in_), r=r, w=w)

    def memset(self, ap, val, w, eng="dve"):
        e = self.nc.vector if eng == "dve" else self.nc.gpsimd
        return self.S.op(eng, lambda: e.memset(ap, val), r=(), w=w)

    def load(self, out, in_, w, q="sp", r=()):
        e = self.S.eo[q]
        return self.S.dma(q, lambda: e.dma_start(out=out, in_=in_), r=r, w=w)

    def store(self, out, in_, r, w, q="sp"):
        e = self.S.eo[q]
        return self.S.dma(q, lambda: e.dma_start(out=out, in_=in_), r=r, w=w, store=True)

    def finish(self):
        global LAST_DINS
        LAST_DINS = list(self.dins.keys())
        self.S.emit()
        self.stack.close()
        return self.nc


def bc_mid(ap2d, k):
    return ap2d.unsqueeze(1).to_broadcast([ap2d.shape[0], k, ap2d.shape[1]])


def load_const(cx, name, dram_ap, shape, dt=F32, q="sp"):
    t = cx.sb(name, shape, dt)
    r = cx.res(name)
    cx.load(t[:], dram_ap, w=r, q=q)
    return t, r


def norm_mod_block(cx, xs, xs_r, c0, c1, Avec, Svec, vec_r, ones_t, ones_r, scr, out_t, out_r, out_c0, mask=None):
    n = c1 - c0
    sq, sq_r, tb, tb_r, rs, rs_r, pb = scr["sq"], scr["sq_r"], scr["tb"], scr["tb_r"], scr["rs"], scr["rs_r"], scr["pb"]
    ps, ps_r = cx.ps[pb], cx.psr[pb]
    cx.act(sq[:, :, 0:n], xs[:, :, c0:c1], AF.Square, r=[xs_r], w=[sq_r])
    for kc in range(8):
        cx.mm(ps[:, 0:n], ones_t[:], sq[:, kc, 0:n], kc == 0, kc == 7, r=[ones_r, sq_r], w=[ps_r])
    cx.act(rs[:, 0:n], ps[:, 0:n], AF.Sqrt, r=[ps_r], w=[rs_r], scale=1.0 / D, bias=EPS)
    cx.recip(rs[:, 0:n], rs[:, 0:n], r=[rs_r], w=[rs_r])
    cx.tt(tb[:, :, 0:n], xs[:, :, c0:c1], bc_mid(rs[:, 0:n], 8), ALU.mult, r=[xs_r, rs_r], w=[tb_r])
    for (a, b, j) in segs(c0, c1):
        for kc in range(8):
            cx.act(out_t[:, kc, out_c0 + a - c0:out_c0 + b - c0], tb[:, kc, a - c0:b - c0], AF.Identity,
                   r=[tb_r, vec_r], w=[out_r], scale=Avec[:, kc, j:j + 1], bias=Svec[:, kc, j:j + 1])
    if mask is not None:
        mt, mr = mask
        cx.tt(out_t[:, :, out_c0:out_c0 + n], out_t[:, :, out_c0:out_c0 + n], bc_mid(mt[:, c0:c1], 8), ALU.mult,
              r=[out_r, mr], w=[out_r], eng="pool")


def make_AS(cx, name, modt, mod_r, g_t, g_r, sc_off, sh_off):
    A = cx.sb(name + "_A", [128, 8, 2], F32)
    r = cx.res(name)
    cx.ts(A[:], modt[:, sc_off:sc_off + 8, :], 1.0, None, ALU.add, None, r=[mod_r], w=[r])
    cx.tt(A[:], A[:], g_t[:, :].unsqueeze(2).to_broadcast([128, 8, 2]), ALU.mult, r=[r, g_r], w=[r])
    return A, modt[:, sh_off:sh_off + 8, :], r


def build_prep(cx=None, io=None):
    own = cx is None
    if own:
        cx = Ctx()
    nc = cx.nc
    xT = cx.din("xT", [D, W], F32)
    cvec = cx.din("cvec", [128, 8, 2], F32)
    wmod = cx.din("w_mod", [DEPTH, D, 6 * D], F32)
    bmod = cx.din("b_mod", [DEPTH, 128, 48], F32)
    g1 = cx.din("g1", [128, 8], F32)
    ones_d = cx.din("ones", [128, 128], F32)
    if own:
        modT = cx.dout("modT", [DEPTH, 128, 48, 2], F32)
        hT = cx.dout("hT", [D, W], BF16)
        modsb = cx.sb("modsb", [128, DEPTH, 48, 2], F32)
        mod_r = cx.res("modsb")
    else:
        modsb, mod_r = io["modsb"], io["mod_r"]

    xs = cx.sb("xs", [128, 8, W], F32)
    xs_r = cx.res("xs")
    xv = xT.rearrange("(kc p) n -> p kc n", p=128)
    cx.load(xs[:], xv, w=xs_r)
    ones_t, ones_r = load_const(cx, "ones_t", ones_d, [128, 128])
    g1_t, g1_r = load_const(cx, "g1_t", g1, [128, 8])
    cv, cv_r = load_const(cx, "cv", cvec, [128, 8, 2])
    bm, bm_r = load_const(cx, "bm", bmod.rearrange("l p o -> p l o"), [128, DEPTH, 48])
    sc = cx.sb("silu_c", [128, 8, 2], F32)
    sc_r = cx.res("silu_c")
    cx.act(sc[:], cv[:], AF.Silu, r=[cv_r], w=[sc_r])
    def emit_hT():
        scr = dict(sq=cx.sb("sq", [128, 8, 512], F32), sq_r=cx.res(), tb=cx.sb("tb", [128, 8, 512], F32), tb_r=cx.res(),
                   rs=cx.sb("rs", [128, 512], F32), rs_r=cx.res(), pb=2)
        A, Sv, vr = make_AS(cx, "as1", modsb[:, 0], mod_r, g1_t, g1_r, 8, 0)
        hb = [cx.sb("hb%d" % i, [128, 8, 512], BF16) for i in range(2)]
        hb_r = [cx.res() for i in range(2)]
        ho_r = [cx.res() for i in range(2)]
        if own:
            hv = hT.rearrange("(kc p) n -> p kc n", p=128)
        for bi, (c0, c1) in enumerate(DISJ):
            i = bi % 2
            scr["pb"] = 2 + (bi % 2)
            norm_mod_block(cx, xs, xs_r, c0, c1, A, Sv, vr, ones_t, ones_r, scr, hb[i], hb_r[i], 0)
            if own:
                cx.store(hv[:, :, c0:c1], hb[i][:, :, 0:c1 - c0], r=[hb_r[i]], w=ho_r[i])
            else:
                io["h_store"](hb[i], hb_r[i], c0, c1, c0)
    NB = 4
    wb = [cx.sb("wmb%d" % i, [128, 8, 512], F32) for i in range(NB)]
    wb_r = [cx.res("wmb%d" % i) for i in range(NB)]
    it = 0
    for l in range(DEPTH):
        for og in range(12):
            i = it % NB
            cx.load(wb[i][:], wmod[l, :, og * 512:(og + 1) * 512].rearrange("(kc p) n -> p kc n", p=128), w=wb_r[i],
                    q=("sp", "act", "pool")[it % 3])
            pb = it % 2
            for o4 in range(4):
                for kc in range(8):
                    cx.mm(cx.ps[pb][:, o4 * 2:o4 * 2 + 2], wb[i][:, kc, o4 * 128:(o4 + 1) * 128], sc[:, kc, :], kc == 0, kc == 7,
                          r=[wb_r[i], sc_r], w=[cx.psr[pb]])
            for o4 in range(4):
                oc = og * 4 + o4
                cx.act(modsb[:, l, oc, :], cx.ps[pb][:, o4 * 2:o4 * 2 + 2], AF.Identity, r=[cx.psr[pb], bm_r], w=[mod_r],
                       bias=bm[:, l, oc:oc + 1])
            it += 1
            if l == 0 and og == 3:
                emit_hT()
    if own:
        mo_r = cx.res("modT_out")
        cx.store(modT.rearrange("l p o j -> p l o j"), modsb[:], r=[mod_r], w=mo_r)
    return cx.finish() if own else None


def pvec(v, nch):
    return np.ascontiguousarray(v.reshape(nch, 128).T)


def local_xT(x_b, ctx_b, j):
    out = np.zeros((W, D), np.float32)
    lo = 2048 * j - HALO
    a, b_ = max(lo, 0), min(lo + LATW, S)
    out[a - lo:b_ - lo] = x_b[a:b_]
    out[CTX0:CTX0 + NCTX] = ctx_b
    return np.ascontiguousarray(out.T)


def run_prog(nc, in_maps):
    res = run_bass_kernel_spmd(nc, in_maps, core_ids=list(range(len(in_maps))))
    return res.results


def prep_inputs(inp, b, j):
    cv = np.stack([inp["c"][b], inp["c_ctx"]], axis=-1)
    return {
        "xT": local_xT(inp["x"][b], inp["ctx"][b], j),
        "cvec": np.ascontiguousarray(cv.reshape(8, 128, 2).transpose(1, 0, 2)),
        "w_mod": inp["w_mod"],
        "b_mod": np.ascontiguousarray(inp["b_mod"].reshape(DEPTH, 48, 128).transpose(0, 2, 1)),
        "g1": pvec(inp["norm1_g"][0], 8),
        "ones": np.ones((128, 128), np.float32),
    }


def build_post(last, cx=None, io=None, sfx="", sfxn=""):
    own = cx is None
    if own:
        cx = Ctx()
    nc = cx.nc
    g2d = cx.din("g2" + sfx, [128, 8], F32)
    ones_d = cx.din("ones", [128, 128], F32)
    w_out = cx.din("w_out" + sfx, [D, D], F32)
    w_four = cx.din("w_four" + sfx, [256, 256], F32)
    w_up = cx.din("w_up" + sfx, [D, 2 * DFF], F32)
    cwd = cx.din("cw" + sfx, [128, 44, 3], F32)
    cbd = cx.din("cb" + sfx, [128, 44], F32)
    w_down = cx.din("w_down" + sfx, [DFF, D], F32)
    maskd = cx.din("mask", [128, W], BF16)
    if own:
        xT = cx.din("xT", [D, W], F32)
        yT = cx.din("yT", [D, W], BF16)
        modT = cx.din("modT", [128, 48, 2], F32)
        xo = cx.dout("xTo", [D, W], F32)
        x_src, x_src_r = xT.rearrange("(kc p) n -> p kc n", p=128), []
        x_dst = xo.rearrange("(kc p) n -> p kc n", p=128)
        if not last:
            modTn = cx.din("modTn", [128, 48, 2], F32)
            hTo = cx.dout("hTo", [D, W], BF16)
    else:
        x_src, x_src_r, x_dst = io["x_src"], io["x_src_r"], io["x_dst"]
    if not last:
        g1nd = cx.din("g1n" + sfxn, [128, 8], F32)

    xs = cx.sb("xs", [128, 8, W], F32)
    xs_r = cx.res("xs")
    cx.load(xs[:], x_src, w=xs_r, r=x_src_r)
    h2 = cx.sb("h2", [128, 8, W], BF16)
    h2_r = [cx.res("h2_%d" % i) for i in range(len(DISJ))]
    ones_t, ones_r = load_const(cx, "ones_t", ones_d, [128, 128])
    if own:
        mod_t, mod_r = load_const(cx, "mod_t", modT, [128, 48, 2])
    else:
        mod_t, mod_r = io["mod_t"], io["mod_r"]
    g2_t, g2_r = load_const(cx, "g2_t", g2d, [128, 8])
    cw_t, cw_r = load_const(cx, "cw_t", cwd, [128, 44, 3])
    cb_t, cb_r = load_const(cx, "cb_t", cbd, [128, 44])
    mk_t, mk_r = load_const(cx, "mk_t", maskd, [128, W], BF16)
    A2, S2, v2r = make_AS(cx, "as2", mod_t, mod_r, g2_t, g2_r, 32, 24)
    G1 = mod_t[:, 16:24, :]
    G2 = mod_t[:, 40:48, :]
    if own:
        yv = yT.rearrange("(kc p) n -> p kc n", p=128)

        def y_load(t_, parts, c0, c1):
            cx.load(t_[:, :, 0:c1 - c0], yv[:, :, c0:c1], w=parts[0])
    else:
        y_load = io["y_load"]

    with ExitStack() as st1:
        def sb1(name, shape, dt):
            return st1.enter_context(nc.sbuf_tensor(cx.uname(name), list(shape), dt))
        wo = sb1("wo", [128, 8, D], BF16)
        wo_r = cx.res("wo")
        cx.load(wo[:], w_out.rearrange("(kc p) n -> p kc n", p=128), w=wo_r, q="pool")
        wf = sb1("wf", [128, 2, 256], BF16)
        wf_r = cx.res("wf")
        cx.load(wf[:], w_four.rearrange("(kc p) n -> p kc n", p=128), w=wf_r, q="pool")
        ysb = [sb1("ysb%d" % i, [128, 8, 512], BF16) for i in range(2)]
        ys_r = [[cx.res() for _ in range(26)] for i in range(2)]
        fo = [sb1("fo%d" % i, [128, 2, 512], BF16) for i in range(2)]
        fo_r = [cx.res() for i in range(2)]
        scr = dict(sq=sb1("sq", [128, 8, 512], F32), sq_r=cx.res(), tb=sb1("tb", [128, 8, 512], F32), tb_r=cx.res(),
                   rs=sb1("rs", [128, 512], F32), rs_r=cx.res(), pb=7)
        pbi = 0
        deferred = [None]
        for bi, (c0, c1) in enumerate(DISJ):
            n = c1 - c0
            i = bi % 2
            y_load(ysb[i], ys_r[i], c0, c1)
            for oc in range(2):
                pb = pbi % 6
                pbi += 1
                for kc in range(2):
                    cx.mm(cx.ps[pb][:, 0:n], wf[:, kc, oc * 128:(oc + 1) * 128], ysb[i][:, 6 + kc, 0:n], kc == 0, kc == 1,
                          r=[wf_r] + ys_r[i], w=[cx.psr[pb]])
                cx.act(fo[i][:, oc, 0:n], cx.ps[pb][:, 0:n], AF.Copy, r=[cx.psr[pb]], w=[fo_r[i]])
            for dc in range(8):
                pb = pbi % 6
                pbi += 1
                for kc in range(8):
                    rhs = ysb[i][:, kc, 0:n] if kc < 6 else fo[i][:, kc - 6, 0:n]
                    cx.mm(cx.ps[pb][:, 0:n], wo[:, kc, dc * 128:(dc + 1) * 128], rhs, kc == 0, kc == 7,
                          r=[wo_r, fo_r[i]] + ys_r[i], w=[cx.psr[pb]])
                for (a, b, j) in segs(c0, c1):
                    cx.stt(xs[:, dc, a:b], cx.ps[pb][:, a - c0:b - c0], G1[:, dc, j:j + 1], xs[:, dc, a:b], ALU.mult, ALU.add,
                           r=[cx.psr[pb], mod_r, xs_r], w=[xs_r])
            if deferred[0] is not None:
                deferred[0]()

            def do_norm(bi=bi, c0=c0, c1=c1):
                scr["pb"] = 6 + (bi % 2)
                norm_mod_block(cx, xs, xs_r, c0, c1, A2, S2, v2r, ones_t, ones_r, scr, h2, h2_r[bi], c0, mask=(mk_t, mk_r))
            deferred[0] = do_norm
        deferred[0]()

    cx.S.barrier()
    with ExitStack() as st2:
        def sb2(name, shape, dt):
            return st2.enter_context(nc.sbuf_tensor(cx.uname(name), list(shape), dt))
        NWB = 4
        NWD = 6
        wg = [sb2("wg%d" % i, [128, 8, 128], BF16) for i in range(NWB)]
        wv = [sb2("wv%d" % i, [128, 8, 128], BF16) for i in range(NWB)]
        wd = [sb2("wd%d" % i, [128, D], BF16) for i in range(NWD)]
        wg_r = [cx.res() for i in range(NWB)]
        wv_r = [cx.res() for i in range(NWB)]
        wd_r = [cx.res() for i in range(NWD)]
        aT = sb2("aT", [128, 22, 512], BF16)
        aT_rs = [cx.res("aT%d" % i) for i in range(22)]
        accg = [sb2("accg%d" % i, [128, 512], F32) for i in range(2)]
        accv = [sb2("accv%d" % i, [128, 512], F32) for i in range(2)]
        sg = [sb2("sg%d" % i, [128, 512], F32) for i in range(2)]
        ag_r = [cx.res() for i in range(2)]
        av_r = [cx.res() for i in range(2)]
        sg_r = [cx.res() for i in range(2)]
        wupv = w_up.rearrange("(kc p) n -> p kc n", p=128)
        it = 0
        for bi, (c0, c1) in enumerate(FFNB):
            n = c1 - c0
            un = n + 2
            hres = [h2_r[k] for k, (d0, d1) in enumerate(DISJ) if d0 < c1 + 1 and d1 > c0 - 1]
            for c in range(NWD):
                cx.load(wd[c][:], w_down[c * 128:(c + 1) * 128, :], w=wd_r[c], q="pool")
            for c in range(22):
                i = it % NWB
                i2 = it % 2
                it += 1
                cx.load(wg[i][:], wupv[:, :, c * 128:(c + 1) * 128], w=wg_r[i], q="pool")
                cx.load(wv[i][:], wupv[:, :, DFF + c * 128:DFF + (c + 1) * 128], w=wv_r[i], q="pool")
                pg, pv = 2 * i2, 2 * i2 + 1
                for kc in range(8):
                    cx.mm(cx.ps[pg][:, 0:un], wg[i][:, kc, :], h2[:, kc, c0 - 1:c1 + 1], kc == 0, kc == 7,
                          r=[wg_r[i]] + hres, w=[cx.psr[pg]])
                for kc in range(8):
                    cx.mm(cx.ps[pv][:, 0:un], wv[i][:, kc, :], h2[:, kc, c0 - 1:c1 + 1], kc == 0, kc == 7,
                          r=[wv_r[i]] + hres, w=[cx.psr[pv]])
                for (pp, acc, acc_r, ch) in ((pg, accg[i2], ag_r[i2], c), (pv, accv[i2], av_r[i2], c + 22)):
                    cx.act(acc[:, 0:n], cx.ps[pp][:, 1:1 + n], AF.Identity, r=[cx.psr[pp], cw_r, cb_r], w=[acc_r],
                           scale=cw_t[:, ch, 1:2], bias=cb_t[:, ch:ch + 1])
                    cx.stt(acc[:, 0:n], cx.ps[pp][:, 0:n], cw_t[:, ch, 0:1], acc[:, 0:n], ALU.mult, ALU.add,
                           r=[cx.psr[pp], cw_r, acc_r], w=[acc_r])
                    cx.stt(acc[:, 0:n], cx.ps[pp][:, 2:2 + n], cw_t[:, ch, 2:3], acc[:, 0:n], ALU.mult, ALU.add,
                           r=[cx.psr[pp], cw_r, acc_r], w=[acc_r])
                cx.act(sg[i2][:, 0:n], accg[i2][:, 0:n], AF.Silu, r=[ag_r[i2]], w=[sg_r[i2]])
                cx.tt(aT[:, c, 0:n], sg[i2][:, 0:n], accv[i2][:, 0:n], ALU.mult, r=[sg_r[i2], av_r[i2]], w=[aT_rs[c]])
            for c in range(22):
                i = c % NWD
                for dc in range(8):
                    cx.mm(cx.ps[dc][:, 0:n], wd[i][:, dc * 128:(dc + 1) * 128], aT[:, c, 0:n], c == 0, c == 21,
                          r=[wd_r[i], aT_rs[c]], w=[cx.psr[dc]])
                if c + NWD < 22:
                    cx.load(wd[i][:], w_down[(c + NWD) * 128:(c + NWD + 1) * 128, :], w=wd_r[i], q="pool")
            for dc in range(8):
                for (a, b, j) in segs(c0, c1):
                    cx.stt(xs[:, dc, a:b], cx.ps[dc][:, a - c0:b - c0], G2[:, dc, j:j + 1], xs[:, dc, a:b], ALU.mult, ALU.add,
                           r=[cx.psr[dc], mod_r, xs_r], w=[xs_r])

    cx.S.barrier()
    xo_r = io["x_dst_r"] if (not own and "x_dst_r" in io) else cx.res("xo")
    cx.store(x_dst, xs[:], r=[xs_r], w=xo_r)
    if not last:
        with ExitStack() as st3:
            def sb3(name, shape, dt):
                return st3.enter_context(nc.sbuf_tensor(cx.uname(name), list(shape), dt))
            if own:
                modn_t, modn_r = load_const(cx, "modn_t", modTn, [128, 48, 2])
            else:
                modn_t, modn_r = io["modn_t"], io["mod_r"]
            g1n_t, g1n_r = load_const(cx, "g1n_t", g1nd, [128, 8])
            A1, S1, v1r = make_AS(cx, "as1n", modn_t, modn_r, g1n_t, g1n_r, 8, 0)
            scr = dict(sq=sb3("sq3", [128, 8, 512], F32), sq_r=cx.res(), tb=sb3("tb3", [128, 8, 512], F32), tb_r=cx.res(),
                       rs=sb3("rs3", [128, 512], F32), rs_r=cx.res(), pb=0)
            ho_r = [cx.res() for i in range(len(DISJ))]
            if own:
                hv = hTo.rearrange("(kc p) n -> p kc n", p=128)
            for bi, (c0, c1) in enumerate(DISJ):
                scr["pb"] = bi % 2
                norm_mod_block(cx, xs, xs_r, c0, c1, A1, S1, v1r, ones_t, ones_r, scr, h2, h2_r[bi], c0)
                if own:
                    cx.store(hv[:, :, c0:c1], h2[:, :, c0:c1], r=[h2_r[bi]], w=ho_r[bi])
                else:
                    io["h_store"](h2, h2_r[bi], c0, c1, 0)
    return cx.finish() if own else None


def post_inputs(inp, l, xT_loc, yT_loc, modT_l, modT_n, mask, last):
    d = {
        "xT": xT_loc, "yT": yT_loc, "modT": modT_l,
        "g2": pvec(inp["norm2_g"][l], 8), "ones": np.ones((128, 128), np.float32),
        "w_out": inp["w_out"][l], "w_four": inp["w_fourier"][l], "w_up": inp["w_up"][l],
        "cw": np.ascontiguousarray(inp["conv_w"][l].reshape(3, 44, 128).transpose(2, 1, 0)),
        "cb": pvec(inp["conv_b"][l], 44), "w_down": inp["w_down"][l], "mask": mask,
    }
    if not last:
        d["modTn"] = modT_n
        d["g1n"] = pvec(inp["norm1_g"][l + 1], 8)
    return d


def col_mask(j):
    m = np.zeros((W,), np.float32)
    lo = 2048 * j - HALO
    a, b_ = max(lo, 0), min(lo + LATW, S)
    m[a - lo:b_ - lo] = 1.0
    m[CTX0:CTX0 + NCTX] = 1.0
    return np.ascontiguousarray(np.broadcast_to(m[None, :], (128, W))).astype(NPBF)


def local_cols(full_T, j):
    R = full_T.shape[0]
    out = np.zeros((R, W), full_T.dtype)
    lo = 2048 * j - HALO
    a, b_ = max(lo, 0), min(lo + LATW, S)
    out[:, a - lo:b_ - lo] = full_T[:, a:b_]
    out[:, CTX0:CTX0 + NCTX] = full_T[:, S:SE]
    return out


TB = [(i * 512, 512) for i in range(16)] + [(S, NCTX)]
NKT = SE // 128


def na_tiles(qb):
    if qb == 0:
        return [(kt, 0, kt) for kt in range(0, 6)]
    if qb == 15:
        return [(kt, 2, kt - 58) for kt in range(58, 64)]
    return [(kt, 1, kt - (4 * qb - 2)) for kt in range(4 * qb - 2, 4 * qb + 6)]


def build_mixer(ctxq, lam_init, debug=False, stop_after=99, cx=None, io=None, sfx=""):
    own = cx is None
    if own:
        cx = Ctx()
    nc = cx.nc
    w_in = cx.din("w_in" + sfx, [D, 640], F32)
    gains = cx.din("gains" + sfx, [128, 4], F32)
    cosd = cx.din("cos", [128, SE], F32)
    sind = cx.din("sin", [128, SE], F32)
    rperm = cx.din("rperm", [128, 128], F32)
    bo64 = cx.din("bo64", [128, 128], F32)
    bo32 = cx.din("bo32", [128, 128], F32)
    lamd = cx.din("lam" + sfx, [64, 4, 32], F32)
    biasd = cx.din("bias" + sfx, [2, 3, 8, 128, 512], F32)
    ones_bf = cx.din("ones_bf", [128, 128], BF16)
    fa_d = cx.din("f_rhs_a", [64, 128], BF16)
    fac_d = cx.din("f_rhs_ac", [64, 128], BF16)
    cs128_d = cx.din("f_cs128", [128, 256], F32)
    tab_d = cx.din("f_tab", [128, 2, 128], F32)
    c64s_d = cx.din("f_cs64s", [128, 2, 64], F32)
    c256_d = cx.din("f_cs256", [128, 2, 2, 256], F32)
    if own:
        hT = cx.din("hT", [D, SE], BF16)
        yT = cx.dout("yT", [256, SE], BF16)
        hv = hT.rearrange("(kc p) n -> p kc n", p=128)

        def h_load(t_, t_r, bi):
            t0_, n_ = TB[bi]
            cx.load(t_[:, :, 0:n_], hv[:, :, t0_:t0_ + n_], w=t_r)

        def y_store(r0, r1, t0_, n_, src, src_r, w):
            cx.store(yT[r0:r1, t0_:t0_ + n_], src, r=src_r, w=w)
    else:
        h_load, y_store = io["h_load"], io["y_store"]

    gn, gn_r = load_const(cx, "gn", gains, [128, 4])
    rp, rp_r = load_const(cx, "rp", rperm, [128, 128])
    b64, b64_r = load_const(cx, "b64", bo64, [128, 128])
    b32, b32_r = load_const(cx, "b32", bo32, [128, 128])
    lm, lm_r = load_const(cx, "lm", lamd, [64, 4, 32])
    onb, onb_r = load_const(cx, "onb", ones_bf, [128, 128], BF16)

    qna = cx.sb("qna", [128, SE], BF16)
    qna1 = cx.sb("qna1", [128, SE], BF16)
    qnz_r = cx.res("qnz")
    cx.memset(qna[64:128, :], 0.0, w=[qnz_r], eng="pool")
    cx.memset(qna1[0:64, :], 0.0, w=[qnz_r], eng="pool")
    kna = cx.sb("kna", [128, SE], BF16)
    vna = cx.sb("vna", [128, NKT, 2, 64], BF16)
    qd = cx.sb("qd", [128, SE], BF16)
    qd2 = cx.sb("qd2", [128, SE], BF16)
    kd = cx.sb("kd", [128, SE], BF16)
    qz_r = cx.res("qz")
    cx.memset(qd[:, :], 0.0, w=[qz_r], eng="pool")
    cx.memset(qd2[:, :], 0.0, w=[qz_r], eng="pool")
    cx.memset(kd[64:128, :], 0.0, w=[qz_r], eng="pool")
    vd = cx.sb("vd", [128, NKT, 128], BF16)
    st_fu = ExitStack()
    fuT = st_fu.enter_context(nc.sbuf_tensor(cx.uname("fuT"), [64, SE], BF16, side="right"))
    nb = len(TB)
    qna_r = [cx.res() for _ in range(nb)]
    kna_r = [cx.res() for _ in range(nb)]
    v_r = [cx.res() for _ in range(nb)]
    qd_r = [cx.res() for _ in range(nb)]
    kd_r = [cx.res() for _ in range(nb)]
    fu_r = [cx.res() for _ in range(nb)]
    vd1_r = cx.res("vd_ones")
    cx.memset(vd[:, :, 64:128], 1.0, w=[vd1_r], eng="pool")

    lt = cx.sb("lt", [64, 2, 32], F32)
    ls = cx.sb("ls", [64, 2], F32)
    nlam = cx.sb("nlam", [64, 1], F32)
    gsub = cx.sb("gsub", [64, 1], F32)
    lt_r, ls_r, nl_r, gs_r = cx.res(), cx.res(), cx.res(), cx.res()
    cx.tt(lt[:, 0, :], lm[:, 0, :], lm[:, 1, :], ALU.mult, r=[lm_r], w=[lt_r])
    cx.tt(lt[:, 1, :], lm[:, 2, :], lm[:, 3, :], ALU.mult, r=[lm_r, lt_r], w=[lt_r])
    cx.S.op("dve", lambda: nc.vector.reduce_sum(out=ls[:], in_=lt[:], axis=AX.X), r=[lt_r], w=[ls_r])
    cx.act(ls[:], ls[:], AF.Exp, r=[ls_r], w=[ls_r])
    cx.tt(nlam[:], ls[:, 1:2], ls[:, 0:1], ALU.subtract, r=[ls_r], w=[nl_r])
    cx.ts(nlam[:], nlam[:], -float(lam_init), None, ALU.add, None, r=[nl_r], w=[nl_r])
    cx.ts(gsub[:], gn[0:64, 3:4], 1.0 - float(lam_init), None, ALU.mult, None, r=[gn_r], w=[gs_r])

    with ExitStack() as st1:
        def sb1(name, shape, dt):
            return st1.enter_context(nc.sbuf_tensor(cx.uname(name), list(shape), dt))
        win = sb1("win", [128, 8, 640], BF16)
        win_r = cx.res("win")
        cx.load(win[:], w_in.rearrange("(kc p) n -> p kc n", p=128), w=win_r, q="pool")
        hb = [sb1("hb%d" % i, [128, 8, 512], BF16) for i in range(2)]
        hb_r = [cx.res() for _ in range(2)]
        cs = [sb1("cs%d" % i, [128, 2, 512], F32) for i in range(2)]
        cs_r = [cx.res() for _ in range(2)]
        xn = sb1("xn", [128, 512], F32)
        t1 = sb1("t1", [128, 512], F32)
        t2 = sb1("t2", [128, 512], F32)
        xn_r, t1_r, t2_r = cx.res(), cx.res(), cx.res()
        sq = [sb1("sq3_%d" % i, [128, 512], F32) for i in range(3)]
        rs = [sb1("rs3_%d" % i, [128, 512], F32) for i in range(3)]
        sq_r = [cx.res() for _ in range(3)]
        rs_r = [cx.res() for _ in range(3)]
        pending = [None]
        for bi, (t0, n) in enumerate(TB):
            i = bi % 2
            h_load(hb[i], hb_r[i], bi)
            cx.load(cs[i][:, 0, 0:n], cosd[:, t0:t0 + n], w=cs_r[i], q="act")
            cx.load(cs[i][:, 1, 0:n], sind[:, t0:t0 + n], w=cs_r[i], q="act")
            for ch in range(3):
                ps, ps_r = cx.ps[ch], cx.psr[ch]
                for kc in range(8):
                    cx.mm(ps[:, 0:n], win[:, kc, ch * 128:(ch + 1) * 128], hb[i][:, kc, 0:n], kc == 0, kc == 7,
                          r=[win_r, hb_r[i]], w=[ps_r])
                cx.act(sq[ch][:, 0:n], ps[:, 0:n], AF.Square, r=[ps_r], w=[sq_r[ch]])
            p4, p4_r = cx.ps[5], cx.psr[5]
            for kc in range(8):
                cx.mm(p4[0:64, 0:n], win[:, kc, 384:448], hb[i][:, kc, 0:n], kc == 0, kc == 7, r=[win_r, hb_r[i]], w=[p4_r])
            cx.act(fuT[:, t0:t0 + n], p4[0:64, 0:n], AF.Copy, r=[p4_r], w=[fu_r[bi]])
            for tt_ in range(n // 128):
                kt = t0 // 128 + tt_
                p5, p5_r = cx.ps[6 + tt_ % 2], cx.psr[6 + tt_ % 2]
                for kc in range(8):
                    cx.mm(p5[:, 0:192], hb[i][:, kc, tt_ * 128:(tt_ + 1) * 128], win[:, kc, 448:640], kc == 0, kc == 7,
                          r=[win_r, hb_r[i]], w=[p5_r])
                cx.copy(vna[:, kt, :, :], p5[:, 0:128].rearrange("p (h e) -> p h e", h=2), r=[p5_r], w=[v_r[bi]], eng="act")
                cx.copy(vd[:, kt, 0:64], p5[:, 128:192], r=[p5_r, vd1_r], w=[v_r[bi]], eng="dve")
            if pending[0] is not None:
                pending[0]()
            for ch in range(3):
                ps, ps_r = cx.ps[ch], cx.psr[ch]
                p2, p2_r = cx.ps[3 + ch % 2], cx.psr[3 + ch % 2]
                bo, bo_r = (b64, b64_r) if ch < 2 else (b32, b32_r)
                cx.mm(p2[:, 0:n], bo[:], sq[ch][:, 0:n], True, True, r=[bo_r, sq_r[ch]], w=[p2_r])
                cx.act(rs[ch][:, 0:n], p2[:, 0:n], AF.Sqrt, r=[p2_r], w=[rs_r[ch]], bias=EPS)
                cx.recip(rs[ch][:, 0:n], rs[ch][:, 0:n], r=[rs_r[ch]], w=[rs_r[ch]])
                if ch == 0:
                    cx.stt(qna[0:64, t0:t0 + n], ps[0:64, 0:n], gn[0:64, 0:1], rs[ch][0:64, 0:n], ALU.mult, ALU.mult,
                           r=[ps_r, gn_r, rs_r[ch], qnz_r], w=[qna_r[bi]])
                    cx.stt(qna1[64:128, t0:t0 + n], ps[64:128, 0:n], gn[64:128, 0:1], rs[ch][64:128, 0:n], ALU.mult, ALU.mult,
                           r=[ps_r, gn_r, rs_r[ch], qnz_r], w=[qna_r[bi]])
                elif ch == 1:
                    cx.stt(kna[:, t0:t0 + n], ps[:, 0:n], gn[:, 1:2], rs[ch][:, 0:n], ALU.mult, ALU.mult,
                           r=[ps_r, gn_r, rs_r[ch]], w=[kna_r[bi]])
                else:
                    cx.stt(xn[:, 0:n], ps[:, 0:n], gn[:, 2:3], rs[ch][:, 0:n], ALU.mult, ALU.mult,
                           r=[ps_r, gn_r, rs_r[ch]], w=[xn_r])

            def tail(bi=bi, t0=t0, n=n, i=i):
                p3, p3_r = cx.ps[5], cx.psr[5]
                cx.mm(p3[:, 0:n], rp[:], xn[:, 0:n], True, True, r=[rp_r, xn_r], w=[p3_r])
                cx.tt(t1[:, 0:n], xn[:, 0:n], cs[i][:, 0, 0:n], ALU.mult, r=[xn_r, cs_r[i]], w=[t1_r], eng="pool")
                cx.tt(t2[:, 0:n], p3[:, 0:n], cs[i][:, 1, 0:n], ALU.mult, r=[p3_r, cs_r[i]], w=[t2_r])
                cx.tt(qd[0:32, t0:t0 + n], t1[0:32, 0:n], t2[0:32, 0:n], ALU.add, r=[t1_r, t2_r, qz_r], w=[qd_r[bi]], eng="pool")
                cx.tt(qd2[32:64, t0:t0 + n], t1[32:64, 0:n], t2[32:64, 0:n], ALU.add, r=[t1_r, t2_r, qz_r], w=[qd_r[bi]], eng="pool")
                cx.tt(kd[0:64, t0:t0 + n], t1[64:128, 0:n], t2[64:128, 0:n], ALU.add, r=[t1_r, t2_r], w=[kd_r[bi]], eng="pool")
            pending[0] = tail
        pending[0]()

    if debug:
        dq_ = cx.dout("dbg_qna", [128, SE], BF16)
        dk_ = cx.dout("dbg_kna", [128, SE], BF16)
        dqd_ = cx.dout("dbg_qd", [64, SE], BF16)
        dkd_ = cx.dout("dbg_kd", [64, SE], BF16)
        dfu_ = cx.dout("dbg_fu", [64, SE], BF16)
        dvn_ = cx.dout("dbg_vna", [128, NKT, 2, 64], BF16)
        dvd_ = cx.dout("dbg_vd", [128, NKT, 128], BF16)
        cx.store(dq_[0:64], qna[0:64, :], r=qna_r, w=cx.res())
        cx.store(dq_[64:128], qna1[64:128, :], r=qna_r, w=cx.res())
        cx.store(dk_, kna[:], r=kna_r, w=cx.res())
        cx.store(dqd_, qd[0:64, :], r=qd_r, w=cx.res())
        cx.store(dkd_, kd[0:64, :], r=kd_r, w=cx.res())
        cx.store(dfu_, fuT[:], r=fu_r, w=cx.res())
        cx.store(dvn_, vna[:], r=v_r, w=cx.res())
        cx.store(dvd_, vd[:], r=v_r + [vd1_r], w=cx.res())
    if stop_after < 2:
        return cx.finish() if own else None
    def kblk(kt):
        return min(kt // 4, 16)

    cx.S.barrier()
    with ExitStack() as st3:
        def sb3(name, shape, dt):
            return st3.enter_context(nc.sbuf_tensor(cx.uname(name), list(shape), dt))
        fa = sb3("fa", [64, 128], BF16)
        fac = sb3("fac", [64, 128], BF16)
        fa_r, fac_r = cx.res(), cx.res()
        cx.load(fa[:], fa_d, w=fa_r)
        cx.load(fac[:], fac_d, w=fac_r)
        c128 = sb3("c128", [128, 256], F32)
        tab = sb3("tab", [128, 2, 128], F32)
        c64s = sb3("c64s", [128, 2, 64], F32)
        c256 = sb3("c256", [128, 2, 2, 256], F32)
        c128_r, tab_r, c64s_r, c256_r = cx.res(), cx.res(), cx.res(), cx.res()
        cx.load(c128[:], cs128_d, w=c128_r)
        cx.load(tab[:], tab_d, w=tab_r)
        cx.load(c64s[:], c64s_d, w=c64s_r)
        cx.load(c256[:], c256_d, w=c256_r)
        KQ = 16
        Aq = sb3("Aq", [128, 64, 2, KQ], F32)
        Aq_r = cx.res()
        Ure = sb3("Ure", [128, KQ, 128], F32)
        Uim = sb3("Uim", [128, KQ, 128], F32)
        Ure_r, Uim_r = cx.res(), cx.res()
        tm = [sb3("tm%d" % i, [128, 2, 128], F32) for i in range(4)]
        tm_r = [cx.res() for _ in range(4)]
        fo = sb3("fo", [KQ, S], BF16)
        fo_r = cx.res()
        fu_all = fu_r[0:16]
        fov = fo[:, :].rearrange("k (m2 m1) -> k m1 m2", m1=128)
        for q in range(64 // KQ):
            for g8 in range(8):
                pb = g8 % 2
                for n2i in range(8):
                    n2 = g8 * 8 + n2i
                    lhsT = fuT[:, n2:S:64]
                    for ri in range(2):
                        cx.mm(cx.ps[pb][:, (n2i * 2 + ri) * KQ:(n2i * 2 + ri + 1) * KQ], lhsT,
                              fa[:, ri * 64 + q * KQ:ri * 64 + (q + 1) * KQ], True, True, r=fu_all + [fa_r], w=[cx.psr[pb]])
                cx.copy(Aq[:, g8 * 8:(g8 + 1) * 8, :, :], cx.ps[pb][:, 0:8 * 2 * KQ].rearrange("p (a b c) -> p a b c", a=8, b=2),
                        r=[cx.psr[pb]], w=[Aq_r], eng=("act" if g8 % 2 else "dve"))
            for k2 in range(KQ // 2):
                pb = 2 + k2 % 2
                for kk in range(2):
                    kc = k2 * 2 + kk
                    lhsT = Aq[:, :, :, kc].rearrange("p a b -> p (a b)")
                    cx.mm(cx.ps[pb][:, kk * 256:(kk + 1) * 256], lhsT, c128[:, :], True, True, r=[Aq_r, c128_r], w=[cx.psr[pb]])
                pv_ = cx.ps[pb][:, :].rearrange("p (k c m) -> p k c m", k=2, c=2)
                Pc, Ps = pv_[:, :, 0, :], pv_[:, :, 1, :]
                TA = bc_mid(tab[:, 0, :], 2)
                TBt = bc_mid(tab[:, 1, :], 2)
                ksl = slice(k2 * 2, k2 * 2 + 2)
                cx.tt(tm[0][:], Pc, TA, ALU.mult, r=[cx.psr[pb], tab_r], w=[tm_r[0]])
                cx.tt(tm[1][:], Ps, TBt, ALU.mult, r=[cx.psr[pb], tab_r], w=[tm_r[1]])
                cx.tt(Ure[:, ksl, :], tm[0][:], tm[1][:], ALU.add, r=[tm_r[0], tm_r[1]], w=[Ure_r], eng="pool")
                cx.tt(tm[2][:], Pc, TBt, ALU.mult, r=[cx.psr[pb], tab_r], w=[tm_r[2]])
                cx.tt(tm[3][:], Ps, TA, ALU.mult, r=[cx.psr[pb], tab_r], w=[tm_r[3]])
                cx.tt(Uim[:, ksl, :], tm[2][:], tm[3][:], ALU.subtract, r=[tm_r[2], tm_r[3]], w=[Uim_r], eng="pool")
            for g in range(16):
                pb = 4 + g % 2
                for mi in range(8):
                    m1 = g * 8 + mi
                    cx.mm(cx.ps[pb][0:KQ, mi * 64:(mi + 1) * 64], Ure[:, :, m1], c64s[:, 0, :], True, False,
                          r=[Ure_r, c64s_r], w=[cx.psr[pb]])
                    cx.mm(cx.ps[pb][0:KQ, mi * 64:(mi + 1) * 64], Uim[:, :, m1], c64s[:, 1, :], False, True,
                          r=[Uim_r, c64s_r], w=[cx.psr[pb]])
                cx.copy(fov[:, g * 8:(g + 1) * 8, :], cx.ps[pb][0:KQ, 0:512].rearrange("p (a b) -> p a b", a=8),
                        r=[cx.psr[pb]], w=[fo_r], eng=("act" if g % 2 else "dve"))
            y_store(192 + q * KQ, 192 + (q + 1) * KQ, 0, S, fo[:, :], [fo_r], cx.stok())
        if ctxq:
            Ac = sb3("Ac", [128, 2, 128], F32)
            Ac_r = cx.res()
            for t_ in range(2):
                cx.mm(cx.ps[6][:, t_ * 128:(t_ + 1) * 128], fuT[:, S + t_ * 128:S + (t_ + 1) * 128], fac[:, :], True, True,
                      r=[fu_r[16], fac_r], w=[cx.psr[6]])
            cx.copy(Ac[:], cx.ps[6][:, 0:256].rearrange("p (t c) -> p t c", t=2), r=[cx.psr[6]], w=[Ac_r])
            k = 0
            for t_ in range(2):
                for ri in range(2):
                    cx.mm(cx.ps[7][0:64, 0:256], Ac[:, t_, ri * 64:(ri + 1) * 64], c256[:, t_, ri, :], k == 0, k == 3,
                          r=[Ac_r, c256_r], w=[cx.psr[7]])
                    k += 1
            fcs = sb3("fcs", [64, 256], BF16)
            fcs_r = cx.res()
            cx.copy(fcs[:], cx.ps[7][0:64, 0:256], r=[cx.psr[7]], w=[fcs_r])
            y_store(192, 256, S, NCTX, fcs[:], [fcs_r], cx.stok())
    st_fu.close()
    if io is not None and "after_fourier" in io:
        io["after_fourier"]()
    cx.S.barrier()
    yv_r = [cx.res() for _ in range(4)]
    with ExitStack() as st2:
        def sb2(name, shape, dt):
            return st2.enter_context(nc.sbuf_tensor(cx.uname(name), list(shape), dt))
        NP = 3
        pT = [sb2("pT%d" % i, [128, 1024], BF16) for i in range(NP)]
        pT_r = [cx.res() for _ in range(NP)]
        bres = sb2("bres", [128, 2, 8, 512], F32)
        bres_r = cx.res()
        cx.load(bres[:, 0], biasd[0, 1].rearrange("v p q -> p v q"), w=bres_r, q="sp")
        cx.load(bres[:, 1], biasd[1, 1].rearrange("v p q -> p v q"), w=bres_r, q="act")
        bt = [sb2("bt%d" % i, [128, 512], F32) for i in range(3)]
        bt_r = [cx.res() for _ in range(3)]
        sbf = [sb2("sbf%d" % i, [128, 512], F32) for i in range(3)]
        sbf_r = [cx.res() for _ in range(3)]
        yst = [sb2("yst%d" % i, [128, 512], BF16) for i in range(2)]
        yst_r = [cx.res() for _ in range(2)]
        rz = sb2("rz", [128, 512], F32)
        rz_r = cx.res()
        rz2 = sb2("rz2", [128, 512], F32)
        rz2_r = cx.res()
        a1 = sb2("a1", [64, 512], F32)
        a2 = sb2("a2", [64, 512], F32)
        a1_r, a2_r = cx.res(), cx.res()
        ydst = [sb2("ydst%d" % i, [64, 512], BF16) for i in range(2)]
        ydst_r = [cx.res() for _ in range(2)]
        b64n = sb2("b64n", [64, 64], F32)
        b64n_r = cx.res()
        cx.copy(b64n[:], b64[0:64, 0:64], r=[b64_r], w=[b64n_r])
        SC_NA = 64 ** -0.5
        SC_D = 32 ** -0.5
        qblocks = list(range(16)) + ([16] if ctxq else [])
        st_ = dict(pc=0, bc=0, sc=0)
        for qi, qb in enumerate(qblocks):
            t0, n = TB[qb]
            ys, ys_r = yst[qi % 2], yst_r[qi % 2]
            for h in range(2):
                tiles = (na_tiles(qb) if qb < 16 else []) + [(64, None, None), (65, None, None)]
                po, po_r = cx.ps[4 + (h % 2) * 2], cx.psr[4 + (h % 2) * 2]
                pz, pz_r = cx.ps[5 + (h % 2) * 2], cx.psr[5 + (h % 2) * 2]
                hs = slice(64 * h, 64 * h + 64)

                def na_score(tile, h=h, hs=hs, t0=t0, n=n, qb=qb):
                    kt, vg, vi = tile
                    pb = st_["pc"] % 4
                    pss, pss_r = cx.ps[pb], cx.psr[pb]
                    pi = st_["pc"] % NP
                    st_["pc"] += 1
                    pt, pt_r = pT[pi], pT_r[pi]
                    cx.mm(pss[:, 0:n], kna[:, kt * 128:(kt + 1) * 128], (qna if h == 0 else qna1)[:, t0:t0 + n], True, True,
                          r=[kna_r[kblk(kt)], qna_r[qb], qnz_r], w=[pss_r])
                    if vg is not None:
                        s_, s_r = sbf[st_["sc"] % 3], sbf_r[st_["sc"] % 3]
                        st_["sc"] += 1
                        if vg == 1:
                            b_ap, b_r = bres[:, h, vi, 0:n], bres_r
                        else:
                            bb, b_r = bt[st_["bc"] % 3], bt_r[st_["bc"] % 3]
                            st_["bc"] += 1
                            cx.load(bb[:], biasd[h, vg, vi], w=b_r, q="sp")
                            b_ap = bb[:, 0:n]
                        cx.stt(s_[:, 0:n], pss[:, 0:n], SC_NA, b_ap, ALU.mult, ALU.add, r=[pss_r, b_r], w=[s_r])
                        cx.act(pt[:, 0:n], s_[:, 0:n], AF.Exp, r=[s_r], w=[pt_r])
                    else:
                        cx.act(pt[:, 0:n], pss[:, 0:n], AF.Exp, r=[pss_r], w=[pt_r], scale=SC_NA)
                    return pi

                LA_N = 2
                nq = [na_score(tiles[u]) for u in range(min(LA_N, len(tiles)))]
                for ti, (kt, vg, vi) in enumerate(tiles):
                    if ti + LA_N < len(tiles):
                        nq.append(na_score(tiles[ti + LA_N]))
                    cur = nq.pop(0)
                    pt, pt_r = pT[cur], pT_r[cur]
                    last = ti == len(tiles) - 1
                    cx.mm(po[:, 0:n], vna[:, kt, :, :].rearrange("p h e -> p (h e)"), pt[:, 0:n], ti == 0, last,
                          r=[v_r[kblk(kt)], pt_r], w=[po_r])
                    cx.mm(pz[:, 0:n], onb[:, :], pt[:, 0:n], ti == 0, last, r=[onb_r, pt_r], w=[pz_r])
                cx.recip(rz[hs, 0:n], pz[hs, 0:n], r=[pz_r], w=[rz_r])
                cx.tt(ys[hs, 0:n], po[hs, 0:n], rz[hs, 0:n], ALU.mult, r=[po_r, rz_r], w=[ys_r])
            y_store(0, 128, t0, n, ys[:, 0:n], [ys_r], yv_r[qi % 2])
        if io is not None and "after_na" in io:
            io["after_na"]()
        for qi, qb in enumerate(qblocks):
            t0, n = TB[qb]
            kts = list(range(NKT)) if qb < 16 else [64, 65]
            po1, po1_r = cx.ps[6], cx.psr[6]
            po2, po2_r = cx.ps[7], cx.psr[7]

            def d_score(kt, t0=t0, n=n, qb=qb):
                pp = st_["pc"] % 3
                pi = st_["pc"] % NP
                st_["pc"] += 1
                ks = slice(kt * 128, (kt + 1) * 128)
                for comp in range(2):
                    qsrc = qd if comp == 0 else qd2
                    cx.mm(cx.ps[2 * pp + comp][:, 0:n], kd[:, ks], qsrc[:, t0:t0 + n], True, True,
                          r=[kd_r[kblk(kt)], qd_r[qb], qz_r], w=[cx.psr[2 * pp + comp]])
                src = cx.ps_all[:, 1024 * pp:1024 * (pp + 1)].rearrange("p (c n) -> p c n", c=2)[:, :, 0:n]
                dst = pT[pi][:, :].rearrange("p (c n) -> p c n", c=2)[:, :, 0:n]
                cx.act(dst, src, AF.Exp, r=[cx.psr[2 * pp], cx.psr[2 * pp + 1]], w=[pT_r[pi]], scale=SC_D)
                return pi

            LA = 2
            queue = [d_score(kts[u]) for u in range(min(LA, len(kts)))]
            for ti, kt in enumerate(kts):
                if ti + LA < len(kts):
                    queue.append(d_score(kts[ti + LA]))
                cur = queue.pop(0)
                last = ti == len(kts) - 1
                pt, pt_r = pT[cur], pT_r[cur]
                cx.mm(po1[:, 0:n], vd[:, kt, :], pt[:, 0:n], ti == 0, last, r=[v_r[kblk(kt)], vd1_r, pt_r], w=[po1_r])
                cx.mm(po2[:, 0:n], vd[:, kt, :], pt[:, 512:512 + n], ti == 0, last, r=[v_r[kblk(kt)], vd1_r, pt_r], w=[po2_r])
            cx.recip(rz[64:128, 0:n], po1[64:128, 0:n], r=[po1_r], w=[rz_r])
            cx.recip(rz2[64:128, 0:n], po2[64:128, 0:n], r=[po2_r], w=[rz2_r])
            cx.tt(a1[:, 0:n], po1[0:64, 0:n], rz[64:128, 0:n], ALU.mult, r=[po1_r, rz_r], w=[a1_r])
            cx.tt(a2[:, 0:n], po2[0:64, 0:n], rz2[64:128, 0:n], ALU.mult, r=[po2_r, rz2_r], w=[a2_r])
            cx.stt(a1[:, 0:n], a2[:, 0:n], nlam[:, 0:1], a1[:, 0:n], ALU.mult, ALU.add, r=[a2_r, nl_r, a1_r], w=[a1_r])
            cx.act(a2[:, 0:n], a1[:, 0:n], AF.Square, r=[a1_r], w=[a2_r])
            p6, p6_r = cx.ps[0], cx.psr[0]
            cx.mm(p6[0:64, 0:n], b64n[:, :], a2[:, 0:n], True, True, r=[b64n_r, a2_r], w=[p6_r])
            cx.act(rz[0:64, 0:n], p6[0:64, 0:n], AF.Sqrt, r=[p6_r], w=[rz_r], bias=EPS)
            cx.recip(rz[0:64, 0:n], rz[0:64, 0:n], r=[rz_r], w=[rz_r])
            yd, yd_r = ydst[qi % 2], ydst_r[qi % 2]
            cx.stt(yd[:, 0:n], a1[:, 0:n], gsub[:, 0:1], rz[0:64, 0:n], ALU.mult, ALU.mult, r=[a1_r, gs_r, rz_r], w=[yd_r])
            y_store(128, 192, t0, n, yd[:, 0:n], [yd_r], yv_r[2 + qi % 2])

    if io is not None and "after_diff" in io:
        io["after_diff"]()
    return cx.finish() if own else None


_CONST = {}


def mixer_consts():
    if _CONST:
        return _CONST
    pos = np.arange(S)
    rows, cols = pos // 64, pos % 64
    inv = (np.float32(10000.0) ** (-np.arange(0, 16, 2, dtype=np.float32) / np.float32(16))).astype(np.float32)
    cos = np.ones((128, SE), np.float32)
    sin = np.zeros((128, SE), np.float32)
    for p in range(128):
        d = p % 32
        part, dd = d // 16, d % 16
        i, half = dd % 8, dd // 8
        ang = ((rows if part == 0 else cols).astype(np.float32) * inv[i]).astype(np.float32)
        cos[p, :S] = np.cos(ang)
        sin[p, :S] = (-np.sin(ang)) if half == 0 else np.sin(ang)
    rperm = np.zeros((128, 128), np.float32)
    for p in range(128):
        q = p + 8 if (p % 16) < 8 else p - 8
        rperm[q, p] = 1.0
    _CONST["cos"], _CONST["sin"], _CONST["rperm"] = cos, sin, rperm
    _CONST["bo64"] = (np.kron(np.eye(2), np.ones((64, 64))) / 64.0).astype(np.float32)
    _CONST["bo32"] = (np.kron(np.eye(4), np.ones((32, 32))) / 32.0).astype(np.float32)
    _CONST["ones_bf"] = np.ones((128, 128), NPBF)
    two_pi = 2.0 * np.pi
    ch = np.arange(64)
    a64 = two_pi * np.outer(ch, ch) / 64.0
    base = np.concatenate([np.cos(a64), -np.sin(a64)], axis=1)
    _CONST["f_rhs_a"] = (base / math.sqrt(S * 64.0)).astype(NPBF)
    _CONST["f_rhs_ac"] = (base / math.sqrt(NCTX * 64.0)).astype(NPBF)
    n1 = np.arange(128)
    a128 = two_pi * np.outer(n1, n1) / 128.0
    _CONST["f_cs128"] = np.concatenate([np.cos(a128), np.sin(a128)], axis=1).astype(np.float32)
    jj = np.arange(128)
    n2, ri = jj // 2, jj % 2
    at = two_pi * np.outer(n2, n1) / float(S)
    Tr, Ti = np.cos(at), np.sin(at)
    TA = np.where(ri[:, None] == 0, Tr, Ti)
    TBm = np.where(ri[:, None] == 0, -Ti, Tr)
    _CONST["f_tab"] = np.ascontiguousarray(np.stack([TA, TBm], axis=1)).astype(np.float32)
    a64s = two_pi * np.outer(n2, ch) / 64.0
    _CONST["f_cs64s"] = np.ascontiguousarray(np.stack([np.cos(a64s), np.sin(a64s)], axis=1)).astype(np.float32)
    n = np.arange(256)
    a256 = two_pi * np.outer(n, n) / 256.0
    cs = np.stack([np.cos(a256), np.sin(a256)], axis=1)
    _CONST["f_cs256"] = np.ascontiguousarray(cs.reshape(2, 128, 2, 256).transpose(1, 0, 2, 3)).astype(np.float32)
    idx = {}
    for g, qb, kts in ((0, 0, range(0, 6)), (1, 1, range(2, 10)), (2, 15, range(58, 64))):
        q = np.arange(512)
        qr, qc = 8 * qb + q // 64, q % 64
        rs = np.clip(qr - 4, 0, 120)
        csx = np.clip(qc - 8, 0, 48)
        for vi, kt in enumerate(kts):
            k = np.arange(128)
            kr, kc = 2 * kt + k // 64, k % 64
            valid = ((kr[:, None] >= rs[None, :]) & (kr[:, None] < rs[None, :] + 8) &
                     (kc[:, None] >= csx[None, :]) & (kc[:, None] < csx[None, :] + 16))
            dr = np.clip(kr[:, None] - qr[None, :] + 7, 0, 14)
            dc = np.clip(kc[:, None] - qc[None, :] + 15, 0, 30)
            idx[(g, vi)] = (valid, dr, dc)
    _CONST["na_idx"] = idx
    return _CONST


def mixer_inputs(inp, l, j, hT_full):
    C = mixer_consts()
    wi = inp["w_in"][l]
    h0, h1 = 2 * j, 2 * j + 1
    cols = []
    for base in (0, 512):
        cols += list(range(base + h0 * 64, base + h0 * 64 + 64)) + list(range(base + h1 * 64, base + h1 * 64 + 64))
    cols += list(range(1536 + j * 64, 1536 + j * 64 + 64)) + list(range(1792 + j * 64, 1792 + j * 64 + 64))
    cols += list(range(2304 + j * 64, 2304 + j * 64 + 64))
    cols += list(range(1024 + h0 * 64, 1024 + h0 * 64 + 64)) + list(range(1024 + h1 * 64, 1024 + h1 * 64 + 64))
    cols += list(range(2048 + j * 64, 2048 + j * 64 + 64))
    gains = np.zeros((128, 4), np.float32)
    gains[:, 0] = np.tile(inp["na_q_g"][l], 2)
    gains[:, 1] = np.tile(inp["na_k_g"][l], 2)
    gains[:, 2] = np.concatenate([inp["diff_q_g"][l]] * 2 + [inp["diff_k_g"][l]] * 2)
    gains[0:64, 3] = inp["diff_subln_g"][l]
    rpb = inp["na_rpb"][l]
    bias = np.zeros((2, 3, 8, 128, 512), np.float32)
    for (g, vi), (valid, dr, dc) in C["na_idx"].items():
        for hh, h in enumerate((h0, h1)):
            bias[hh, g, vi] = np.where(valid, rpb[h][dr, dc], np.float32(NEG))
    d = {
        "hT": hT_full, "w_in": np.ascontiguousarray(wi[:, cols]), "gains": gains,
        "lam": np.ascontiguousarray(np.broadcast_to(inp["diff_lambda"][l][None], (64, 4, 32))).astype(np.float32),
        "bias": bias,
    }
    for k in ("cos", "sin", "rperm", "bo64", "bo32", "ones_bf", "f_rhs_a", "f_rhs_ac", "f_cs128", "f_tab", "f_cs64s", "f_cs256"):
        d[k] = C[k]
    return d


RG = [[0, 1, 2, 3], [4, 5, 6, 7]]
HS = 2048 + NCTX
YW = 2 + S + 2 + NCTX
YCTX0 = S + 4


def build_fused(stop=99):
    cx = Ctx()
    nc = cx.nc
    modsb = cx.sb("modsb", [128, DEPTH, 48, 2], F32)
    mod_r = cx.res("modsb")
    xo = cx.dout("xTo", [D, W], F32)
    xscr = cx.dram("xscr", [D, W], F32)
    xscr_r = cx.res("xscr")
    NYC, YR = 8, 32
    HCW = [512, 512, 512, 512, NCTX]
    hsend = [[cx.dram("hsend%d_%d" % (l, k), [D, HCW[k]], BF16) for k in range(5)] for l in range(DEPTH)]
    hgath = [[cx.dram("hgath%d_%d" % (l, k), [4 * D, HCW[k]], BF16) for k in range(5)] for l in range(DEPTH)]
    ysend = [[cx.dram("ysend%d_%d" % (l, k), [YR, YW], BF16) for k in range(NYC)] for l in range(DEPTH)]
    ygath = [[cx.dram("ygath%d_%d" % (l, k), [4 * YR, YW], BF16) for k in range(NYC)] for l in range(DEPTH)]
    yloc = [[cx.dram("yloc%d_%d" % (l, k), [4 * YR, LATW], BF16) for k in range(NYC)] for l in range(DEPTH)]
    hsend_r = [[cx.res() for k in range(5)] for l in range(DEPTH)]
    hgath_r = [[cx.res() for k in range(5)] for l in range(DEPTH)]
    ysend_r = [[cx.res() for k in range(NYC)] for l in range(DEPTH)]
    ygath_r = [[cx.res() for k in range(NYC)] for l in range(DEPTH)]
    yloc_r = [[cx.res() for k in range(NYC)] for l in range(DEPTH)]
    dyn = {}

    def pre_emit():
        dyn["r"] = nc.sync.snap((nc.sync.partition_id() % 4) * 2048, min_val=0, max_val=3 * 2048)
    cx.S.pre_emit = pre_emit

    def gather(src, src_r, dst, dst_r):
        cx.S.dma("pool", lambda: nc.gpsimd.collective_compute("AllGather", ALU.bypass, replica_groups=RG,
                                                            ins=[src.ap().opt()], outs=[dst.ap().opt()]),
                 r=[src_r], w=dst_r, inc=1)

    def mk_h_store(l):
        sent = [False] * 5

        def h_store(src, src_r, c0, c1, base):
            a, b_ = max(c0, HALO), min(c1, HALO + 2048)
            while a < b_:
                k = (a - HALO) // 512
                e = min(b_, HALO + (k + 1) * 512)
                hv = hsend[l][k].ap().rearrange("(kc p) n -> p kc n", p=128)
                cx.store(hv[:, :, a - HALO - 512 * k:e - HALO - 512 * k], src[:, :, a - base:e - base], r=[src_r], w=hsend_r[l][k])
                a = e
            a, b_ = max(c0, CTX0), min(c1, CTX0 + NCTX)
            if a < b_:
                hv = hsend[l][4].ap().rearrange("(kc p) n -> p kc n", p=128)
                cx.store(hv[:, :, a - CTX0:b_ - CTX0], src[:, :, a - base:b_ - base], r=[src_r], w=hsend_r[l][4], q="act")
            for k in range(4):
                end_col = HALO + (k + 1) * 512
                if not sent[k] and c1 >= end_col:
                    sent[k] = True
                    gather(hsend[l][k], hsend_r[l][k], hgath[l][k], hgath_r[l][k])
        return h_store

    def mk_h_load(l):
        def h_load(t_, t_r, bi):
            t0_, n_ = TB[bi]
            if bi < 16:
                rnk, k = bi // 4, bi % 4
                cx.load(t_[:, :, 0:n_], hgath[l][k].ap()[rnk * D:(rnk + 1) * D, :].rearrange("(kc p) n -> p kc n", p=128), w=t_r,
                        r=[hgath_r[l][k]])
            else:
                cx.load(t_[:, :, 0:n_], hsend[l][4].ap().rearrange("(kc p) n -> p kc n", p=128), w=t_r, r=[hsend_r[l][4]])
        return h_load

    def mk_y_store(l):
        def y_store(r0, r1, t0_, n_, src, src_r, w):
            c = 2 + t0_ if t0_ < S else YCTX0 + (t0_ - S)
            rr = r0
            while rr < r1:
                k = rr // YR
                e = min(r1, (k + 1) * YR)
                cx.store(ysend[l][k].ap()[rr - k * YR:e - k * YR, c:c + n_], src[rr - r0:e - r0], r=src_r, w=ysend_r[l][k],
                         q=("sp" if k % 2 == 0 else "act"))
                rr = e
        return y_store

    def y_exchange(l, ks):
        for k in ks:
            gather(ysend[l][k], ysend_r[l][k], ygath[l][k], ygath_r[l][k])
        for k in ks:
            cx.S.dma("sp", (lambda l=l, k=k: nc.sync.dma_start(
                out=yloc[l][k].ap(), in_=ygath[l][k].ap()[:, bass.ds(dyn["r"], LATW)])), r=[ygath_r[l][k]], w=yloc_r[l][k])

    def mk_y_load(l, last):
        def y_load(t_, parts, c0, c1):
            n = c1 - c0
            cx.memset(t_[:, :, 0:n], 0.0, w=parts, eng="pool")
            pieces = []
            a, b_ = max(c0, 0), min(c1, LATW)
            if a < b_:
                pieces.append((a, b_, True))
            a, b_ = max(c0, CTX0), min(c1, CTX0 + NCTX)
            if a < b_ and not last:
                pieces.append((a, b_, False))
            qi = 0
            for (a, b_, lat) in pieces:
                dsl = slice(a - c0, b_ - c0)
                for k in range(NYC):
                    if lat:
                        srcv, csl, sr = yloc[l][k].ap(), slice(a, b_), yloc_r[l][k]
                    else:
                        srcv, csl, sr = ygath[l][k].ap(), slice(YCTX0 + a - CTX0, YCTX0 + b_ - CTX0), ygath_r[l][k]
                    v = srcv.rearrange("(r q) n -> q r n", q=YR)
                    if k < 4:
                        specs = [(t_[32 * k:32 * k + 32, 0:4, dsl], slice(0, 4))]
                    else:
                        kc0 = 4 if k < 6 else 6
                        p0 = 32 * (k % 2)
                        specs = [(t_[p0:p0 + 32, kc0:kc0 + 2, dsl], slice(0, 4, 2)),
                                 (t_[64 + p0:64 + p0 + 32, kc0:kc0 + 2, dsl], slice(1, 4, 2))]
                    for (dst, rsl) in specs:
                        cx.load(dst, v[:, rsl, csl], w=parts[qi], r=[sr], q=("sp" if qi % 2 == 0 else "act"))
                        qi += 1
        return y_load

    cx.begin_phase()
    build_prep(cx=cx, io=dict(modsb=modsb, mod_r=mod_r, h_store=mk_h_store(0)))
    cx.end_phase()
    zt = cx.sb("zt", [128, 4], BF16)
    zt_r = cx.res("zt")
    cx.memset(zt[:], 0.0, w=[zt_r], eng="pool")
    for l in range(DEPTH):
        last = l == DEPTH - 1
        sfx = "_l%d" % l
        lam_init = 0.8 - 0.6 * math.exp(-0.3 * l)
        cx.begin_phase()
        for k in range(NYC):
            for c in (0, S + 2):
                cx.store(ysend[l][k].ap()[:, c:c + 2], zt[0:YR, 0:2], r=[zt_r], w=ysend_r[l][k])
        mio = dict(h_load=mk_h_load(l), y_store=mk_y_store(l),
                   after_fourier=(lambda l=l: y_exchange(l, range(6, 8))), after_na=(lambda l=l: y_exchange(l, range(0, 4))))
        build_mixer(not last, lam_init, cx=cx, io=mio, sfx=sfx)
        cx.end_phase()
        y_exchange(l, range(4, 6))
        cx.begin_phase()
        xin = cx.din("xT", [D, W], F32) if l == 0 else xscr.ap()
        io = dict(x_src=xin.rearrange("(kc p) n -> p kc n", p=128), x_src_r=([] if l == 0 else [xscr_r]),
                  x_dst=(xo if last else xscr.ap()).rearrange("(kc p) n -> p kc n", p=128),
                  mod_t=modsb[:, l], mod_r=mod_r, y_load=mk_y_load(l, last))
        if not last:
            io["x_dst_r"] = xscr_r
            io["modn_t"] = modsb[:, l + 1]
            io["h_store"] = mk_h_store(l + 1)
        build_post(last, cx=cx, io=io, sfx=sfx, sfxn="_l%d" % (l + 1))
        cx.end_phase()
    nc_ = cx.finish()
    print("fused program: ops=%d sems=%d" % (len(cx.S.ops), cx.S.nsem))
    return nc_


def fused_inputs(inp, b, j):
    d = prep_inputs(inp, b, j)
    d["mask"] = col_mask(j)
    for l in range(DEPTH):
        sfx = "_l%d" % l
        m = mixer_inputs(inp, l, j, None)
        for k in ("w_in", "gains", "lam", "bias"):
            d[k + sfx] = m[k]
        for k in ("cos", "sin", "rperm", "bo64", "bo32", "ones_bf", "f_rhs_a", "f_rhs_ac", "f_cs128", "f_tab", "f_cs64s", "f_cs256"):
            d[k] = m[k]
        p = post_inputs(inp, l, None, None, None, None, None, True)
        for k in ("g2", "w_out", "w_four", "w_up", "cw", "cb", "w_down"):
            d[k + sfx] = p[k]
        if l + 1 < DEPTH:
            d["g1n_l%d" % (l + 1)] = pvec(inp["norm1_g"][l + 1], 8)
    return d


def kernel(**inputs):
    inp = {k: np.asarray(v) for k, v in inputs.items()}
    cores = [(b, j) for b in range(2) for j in range(4)]
    res = run_prog(build_fused(), [fused_inputs(inp, b, j) for (b, j) in cores])
    out = np.zeros((2, S, D), np.float32)
    for ci, (b, j) in enumerate(cores):
        out[b, 2048 * j:2048 * j + 2048] = np.asarray(res[ci]["xTo"])[:, HALO:HALO + 2048].T
    return out


def kernel_unfused(**inputs):
    inp = {k: np.asarray(v) for k, v in inputs.items()}
    cores = [(b, j) for b in range(2) for j in range(4)]
    r0 = run_prog(build_prep(), [prep_inputs(inp, b, j) for (b, j) in cores])
    modT = [np.asarray(r["modT"]) for r in r0]
    hloc = [np.asarray(r["hT"]) for r in r0]
    xloc = [local_xT(inp["x"][b], inp["ctx"][b], j) for (b, j) in cores]
    masks = [col_mask(j) for j in range(4)]
    for l in range(DEPTH):
        last = l == DEPTH - 1
        lam_init = 0.8 - 0.6 * math.exp(-0.3 * l)
        hfull = []
        for b in range(2):
            parts = [hloc[4 * b + j][:, HALO:HALO + 2048] for j in range(4)] + [hloc[4 * b][:, CTX0:CTX0 + NCTX]]
            hfull.append(np.ascontiguousarray(np.concatenate(parts, axis=1)))
        rm = run_prog(build_mixer(not last, lam_init), [mixer_inputs(inp, l, j, hfull[b]) for (b, j) in cores])
        yfull = []
        for b in range(2):
            y = np.zeros((D, SE), NPBF)
            ncol = SE if not last else S
            for j in range(4):
                yc = np.asarray(rm[4 * b + j]["yT"])
                y[128 * j:128 * j + 128, :ncol] = yc[0:128, :ncol]
                y[512 + 64 * j:512 + 64 * j + 64, :ncol] = yc[128:192, :ncol]
                y[768 + 64 * j:768 + 64 * j + 64, :ncol] = yc[192:256, :ncol]
            yfull.append(y)
        ims = []
        for ci, (b, j) in enumerate(cores):
            ims.append(post_inputs(inp, l, xloc[ci], local_cols(yfull[b], j), np.ascontiguousarray(modT[ci][l]),
                                   None if last else np.ascontiguousarray(modT[ci][l + 1]), masks[j], last))
        rp = run_prog(build_post(last), ims)
        xloc = [np.asarray(r["xTo"]) for r in rp]
        if not last:
            hloc = [np.asarray(r["hTo"]) for r in rp]
    out = np.zeros((2, S, D), np.float32)
    for ci, (b, j) in enumerate(cores):
        out[b, 2048 * j:2048 * j + 2048] = xloc[ci][:, HALO:HALO + 2048].T
    return out
```
